# Optimizing a Trainium2 kernel written in Bass

```python
import math
import jax, jax.numpy as jnp
from jax import lax
import numpy as np

D_MODEL = 2048
BATCH = 1
SEQ = 16384
DEPTH = 4

N_MEM = 256
N_A_LAYERS = DEPTH // 2
N_B_LAYERS = DEPTH - N_A_LAYERS
HEAD_DIM = 128
MIX_WIDTH = D_MODEL
MEM_HEADS = 4
MEM_WIDTH = MEM_HEADS * HEAD_DIM
BRANCH_WIDTH = MIX_WIDTH - MEM_WIDTH
POOL_WINDOWS = (2, 4, 8, 16)
N_POOL_GROUPS = len(POOL_WINDOWS)
POOL_GROUP = BRANCH_WIDTH // N_POOL_GROUPS
MOBA_HEADS = BRANCH_WIDTH // HEAD_DIM
MOBA_BLOCK = 256
MOBA_TOPK = 3
Q_CHUNK = 64
IN_WIDTH = 2 * BRANCH_WIDTH + 2 * MEM_WIDTH
DEEPNORM_ALPHA = (2 * DEPTH) ** 0.25
DEEPNORM_BETA = (8 * DEPTH) ** -0.25
LN_EPS = 1e-5

kernel_name = "yoco_pool_moba_hybrid"


def layer_norm(x, g, b):
    xf = x.astype(jnp.float32)
    mu = xf.mean(-1, keepdims=True)
    var = jnp.square(xf - mu).mean(-1, keepdims=True)
    return ((xf - mu) * lax.rsqrt(var + LN_EPS) * g + b).astype(x.dtype)


def multiscale_pool(u, pool_w, pool_scale):
    B, S, _ = u.shape
    uf = u.astype(jnp.float32)
    cs = jnp.concatenate([jnp.zeros((B, 1, BRANCH_WIDTH), jnp.float32),
                          jnp.cumsum(uf, axis=1)], axis=1)
    t = jnp.arange(S)
    outs = []
    for g, w in enumerate(POOL_WINDOWS):
        sl = slice(g * POOL_GROUP, (g + 1) * POOL_GROUP)
        csg = cs[:, :, sl]
        lo = jnp.maximum(t + 1 - w, 0)
        cnt = (t + 1 - lo).astype(jnp.float32)
        outs.append((csg[:, 1:] - csg[:, lo]) / cnt[None, :, None] - uf[:, :, sl])
    pooled = jnp.stack(outs, axis=2).astype(u.dtype)
    mixed = jnp.einsum('bsgc,gcd->bsgd', pooled, pool_w).reshape(B, S, BRANCH_WIDTH)
    return mixed * pool_scale


def memory_attention(mq, mem, w_mem_kv):
    B, S, _ = mq.shape
    M = mem.shape[1]
    mk, mv = jnp.split(mem @ w_mem_kv, 2, axis=-1)
    q = mq.reshape(B, S, MEM_HEADS, HEAD_DIM)
    k = mk.reshape(B, M, MEM_HEADS, HEAD_DIM)
    v = mv.reshape(B, M, MEM_HEADS, HEAD_DIM)
    s = jnp.einsum('bshd,bmhd->bhsm', q, k).astype(jnp.float32) * (HEAD_DIM ** -0.5)
    p = jax.nn.softmax(s, axis=-1).astype(v.dtype)
    return jnp.einsum('bhsm,bmhd->bshd', p, v).reshape(B, S, MEM_WIDTH)


def moba_shared_kv(h, w_kv):
    B, S, _ = h.shape
    k, v = jnp.split(h @ w_kv, 2, axis=-1)
    nb = -(-S // MOBA_BLOCK)
    pad = nb * MOBA_BLOCK - S
    k = k.reshape(B, S, MOBA_HEADS, HEAD_DIM).transpose(0, 2, 1, 3)
    v = v.reshape(B, S, MOBA_HEADS, HEAD_DIM).transpose(0, 2, 1, 3)
    k = jnp.pad(k, ((0, 0), (0, 0), (0, pad), (0, 0)))
    v = jnp.pad(v, ((0, 0), (0, 0), (0, pad), (0, 0)))
    k_blocks = k.reshape(B, MOBA_HEADS, nb, MOBA_BLOCK, HEAD_DIM)
    v_blocks = v.reshape(B, MOBA_HEADS, nb, MOBA_BLOCK, HEAD_DIM)
    k_mean = k_blocks.astype(jnp.float32).mean(axis=3).astype(k.dtype)
    return k_blocks, v_blocks, k_mean


def moba_attention(q_in, k_blocks, v_blocks, k_mean):
    B, S, _ = q_in.shape
    H = MOBA_HEADS
    nb = k_blocks.shape[2]
    n_sel = min(MOBA_TOPK, nb)
    scale = HEAD_DIM ** -0.5
    q = q_in.reshape(B, S, H, HEAD_DIM).transpose(0, 2, 1, 3)
    b_idx = jnp.arange(B)[:, None, None, None]
    h_idx = jnp.arange(H)[None, :, None, None]
    blk_ids = jnp.arange(nb)
    blk_offs = jnp.arange(MOBA_BLOCK)

    def chunk(c):
        start = c * Q_CHUNK
        qc = lax.dynamic_slice_in_dim(q, start, Q_CHUNK, axis=2)
        own = start // MOBA_BLOCK
        q_pos = start + jnp.arange(Q_CHUNK)
        gate = jnp.einsum('bhqd,bhnd->bhqn', qc, k_mean).astype(jnp.float32)
        gate = jnp.where(blk_ids < own, gate, -jnp.inf)
        _, sel = lax.top_k(gate, n_sel)
        valid = sel < own
        kg = k_blocks[b_idx, h_idx, sel]
        vg = v_blocks[b_idx, h_idx, sel]
        s_sel = jnp.einsum('bhqd,bhqjkd->bhqjk', qc, kg).astype(jnp.float32) * scale
        s_sel = jnp.where(valid[..., None], s_sel, -jnp.inf)
        s_sel = s_sel.reshape(B, H, Q_CHUNK, n_sel * MOBA_BLOCK)
        k_own = lax.dynamic_index_in_dim(k_blocks, own, axis=2, keepdims=False)
        v_own = lax.dynamic_index_in_dim(v_blocks, own, axis=2, keepdims=False)
        s_own = jnp.einsum('bhqd,bhkd->bhqk', qc, k_own).astype(jnp.float32) * scale
        k_pos = own * MOBA_BLOCK + blk_offs
        s_own = jnp.where(k_pos[None, :] <= q_pos[:, None], s_own, -jnp.inf)
        p = jax.nn.softmax(jnp.concatenate([s_sel, s_own], axis=-1), axis=-1).astype(v_blocks.dtype)
        p_sel = p[..., :n_sel * MOBA_BLOCK].reshape(B, H, Q_CHUNK, n_sel, MOBA_BLOCK)
        p_own = p[..., n_sel * MOBA_BLOCK:]
        return (jnp.einsum('bhqjk,bhqjkd->bhqd', p_sel, vg)
                + jnp.einsum('bhqk,bhkd->bhqd', p_own, v_own))

    out = lax.map(chunk, jnp.arange(S // Q_CHUNK))
    return out.transpose(1, 0, 3, 2, 4).reshape(B, S, H * HEAD_DIM)


def setup_inputs(seed: int = 0) -> dict:
    key = jax.random.key(seed)
    ks = jax.random.split(key, 10)
    f32 = jnp.float32
    nrm = jax.random.normal
    x = nrm(ks[0], (BATCH, SEQ, D_MODEL), f32)
    mem = nrm(ks[1], (BATCH, N_MEM, D_MODEL), f32)
    w_in = nrm(ks[2], (DEPTH, D_MODEL, IN_WIDTH), f32) * D_MODEL ** -0.5
    w_out = nrm(ks[3], (DEPTH, MIX_WIDTH, D_MODEL), f32) * (MIX_WIDTH ** -0.5 * DEEPNORM_BETA)
    w_mem_kv = nrm(ks[4], (DEPTH, D_MODEL, 2 * MEM_WIDTH), f32) * D_MODEL ** -0.5
    ln_g = 1.0 + 0.02 * nrm(ks[5], (DEPTH, D_MODEL), f32)
    ln_b = 0.02 * nrm(ks[6], (DEPTH, D_MODEL), f32)
    pool_w = nrm(ks[7], (N_A_LAYERS, N_POOL_GROUPS, POOL_GROUP, POOL_GROUP), f32) * POOL_GROUP ** -0.5
    pool_scale = 1.0 + 0.02 * nrm(ks[8], (N_A_LAYERS, BRANCH_WIDTH), f32)
    w_kv_shared = nrm(ks[9], (D_MODEL, 2 * BRANCH_WIDTH), f32) * D_MODEL ** -0.5
    return {"x": x, "mem": mem, "w_in": w_in, "w_out": w_out, "w_mem_kv": w_mem_kv,
            "ln_g": ln_g, "ln_b": ln_b, "pool_w": pool_w, "pool_scale": pool_scale,
            "w_kv_shared": w_kv_shared}


def reference(x, mem, w_in, w_out, w_mem_kv, ln_g, ln_b, pool_w, pool_scale, w_kv_shared):
    h = x
    split_at = [BRANCH_WIDTH, 2 * BRANCH_WIDTH, 2 * BRANCH_WIDTH + MEM_WIDTH]
    for i in range(DEPTH):
        if i == N_A_LAYERS:
            k_blocks, v_blocks, k_mean = moba_shared_kv(h, w_kv_shared)
        u = h @ w_in[i]
        branch_in, gate_b, mem_q, gate_m = jnp.split(u, split_at, axis=-1)
        if i < N_A_LAYERS:
            branch = multiscale_pool(branch_in, pool_w[i], pool_scale[i])
        else:
            branch = moba_attention(branch_in, k_blocks, v_blocks, k_mean)
        mem_o = memory_attention(mem_q, mem, w_mem_kv[i])
        mixed = jnp.concatenate([branch * jax.nn.silu(gate_b), mem_o * jax.nn.silu(gate_m)], axis=-1)
        y = mixed @ w_out[i]
        h = layer_norm(DEEPNORM_ALPHA * h + y, ln_g[i], ln_b[i])
    return h
```

```python
import contextlib
import os
import numpy as np
import concourse.bass as bass
import concourse.mybir as mybir
from concourse.bass_utils import run_bass_kernel_spmd

F32 = mybir.dt.float32
BF16 = mybir.dt.bfloat16
U8 = mybir.dt.uint8
ALU = mybir.AluOpType
AF = mybir.ActivationFunctionType
AX = mybir.AxisListType

NCORES = 8
D = 2048
SEQ = 16384
DEPTH = 4
NBLK = 8
BLK = 256
HALO = 32
BH = BLK + HALO
T0 = NBLK * BH
T1 = NBLK * BLK
NH = 12
NSLOT = 64
ALPHA = float((2 * DEPTH) ** 0.25)
LN_EPS = 1e-5
SCALE = float(128 ** -0.5)
POOL_W = (2, 4, 8, 16)
NEG = -1.0e30
ENGS = ("pe", "act", "dve", "pool", "sp")


class _Op:
    __slots__ = ("eng", "fn", "dma", "deps", "signal", "sem", "val", "qidx", "cc")

    def __init__(self, eng, fn, dma, cc=False):
        self.cc = cc
        self.eng = eng
        self.fn = fn
        self.dma = dma
        self.deps = {}
        self.signal = False
        self.sem = None
        self.val = 0
        self.qidx = -1


class Prog:
    ND = 8
    ROT = 30000

    def __init__(self, nc):
        self.nc = nc
        self.ops = []
        self.last_write = {}
        self.readers = {}

    def op(self, eng, fn, reads=(), writes=(), dma=False, cc=False):
        o = _Op(eng, fn, dma or cc, cc)
        for k in reads:
            w = self.last_write.get(k)
            if w is not None:
                o.deps[w] = True
        for k in writes:
            w = self.last_write.get(k)
            if w is not None:
                o.deps.setdefault(w, False)
            for r in self.readers.get(k, ()):
                if r is not o:
                    o.deps.setdefault(r, False)
        for k in reads:
            self.readers.setdefault(k, set()).add(o)
        for k in writes:
            self.last_write[k] = o
            self.readers[k] = set()
        self.ops.append(o)
        return o

    def barrier(self):
        lastc = {}
        dmas = {}
        for o in self.ops:
            if o.fn is None:
                continue
            if o.dma:
                dmas.setdefault(o.eng, []).append(o)
            else:
                lastc[o.eng] = o
        for e in ENGS:
            b = _Op(e, None, False)
            for e2, o in lastc.items():
                if e2 != e:
                    b.deps[o] = True
            for e2, lst in dmas.items():
                for o in lst[-(self.ND + 2):]:
                    b.deps[o] = True
                for o in lst:
                    if o.cc:
                        b.deps[o] = True
            self.ops.append(b)
        self.last_write = {}
        self.readers = {}

    def emit(self, stack):
        nc = self.nc
        ops = self.ops
        per_eng = {e: [o for o in ops if o.eng == e] for e in ENGS}
        for o in ops:
            need = {}
            for d, raw in o.deps.items():
                if d.dma:
                    need[d] = True
                elif d.eng == o.eng:
                    if o.eng == "pe":
                        continue
                    if o.dma or raw:
                        need[d] = True
                else:
                    need[d] = True
            o.deps = need
            for d in need:
                d.signal = True
        nsem = [0]

        def newsem(name):
            nsem[0] += 1
            return stack.enter_context(nc.semaphore(name))

        for e in ENGS:
            cnt = 0
            sem = None
            gen = 0
            nd = 0
            ncc = 0
            ccsem = None
            pool = None
            for o in per_eng[e]:
                if o.fn is None:
                    continue
                if o.cc:
                    if ccsem is None:
                        ccsem = newsem(f"cc_{e}")
                    ncc += 1
                    o.sem = ccsem
                    o.val = ncc
                elif o.dma:
                    if pool is None:
                        pool = [newsem(f"d_{e}_{j}") for j in range(self.ND)]
                    o.qidx = nd
                    o.sem = pool[nd % self.ND]
                    o.val = 16 * (nd // self.ND + 1)
                    nd += 1
                elif o.signal:
                    if sem is None or cnt >= self.ROT:
                        sem = newsem(f"c_{e}_{gen}")
                        gen += 1
                        cnt = 0
                    cnt += 1
                    o.sem = sem
                    o.val = cnt
        final_dma = {e: [o for o in per_eng[e] if o.dma and not o.cc] for e in ENGS}
        block = stack.enter_context(nc.Block())
        ND = self.ND

        def run_engine(e, eng):
            known = {}
            dmas = final_dma[e]
            for o in per_eng[e]:
                waits = {}
                for d in o.deps:
                    if known.get(d.sem, 0) >= d.val:
                        continue
                    if waits.get(d.sem, 0) < d.val:
                        waits[d.sem] = d.val
                if o.dma and not o.cc and o.qidx >= ND:
                    p = dmas[o.qidx - ND]
                    if known.get(p.sem, 0) < p.val and waits.get(p.sem, 0) < p.val:
                        waits[p.sem] = p.val
                for s, v in waits.items():
                    eng.wait_ge(s, v)
                    known[s] = v
                if o.fn is None:
                    continue
                ins = o.fn(eng)
                if o.cc:
                    ins.then_inc(o.sem, 1)
                elif o.dma:
                    ins.then_inc(o.sem, 16)
                elif o.signal:
                    ins.then_inc(o.sem, 1)
            if e == "sp":
                for q in ENGS:
                    last = {}
                    for o in final_dma[q]:
                        last[o.sem] = max(last.get(o.sem, 0), o.val)
                    for s, v in last.items():
                        if known.get(s, 0) < v:
                            eng.wait_ge(s, v)

        @block.tensor
        def _(eng):
            run_engine("pe", eng)

        @block.scalar
        def _(eng):
            run_engine("act", eng)

        @block.vector
        def _(eng):
            run_engine("dve", eng)

        @block.gpsimd
        def _(eng):
            run_engine("pool", eng)

        @block.sync
        def _(eng):
            run_engine("sp", eng)
        return nsem[0]


_DTSIZE = {F32: 4, BF16: 2, U8: 1}


class Arena:
    def __init__(self, ap, nbytes):
        self.ap = ap
        self.nbytes = nbytes
        self.off = 0

    def alloc(self, free_shape, dt):
        n = int(np.prod(free_shape)) * _DTSIZE[dt]
        off = (self.off + 63) // 64 * 64
        assert off + n <= self.nbytes, f"arena overflow {off + n} > {self.nbytes}"
        self.off = off + n
        v = self.ap[:, off:off + n].bitcast(dt)
        if len(free_shape) > 1:
            names = [chr(ord("a") + i) for i in range(len(free_shape))]
            pat = "p (" + " ".join(names) + ") -> p " + " ".join(names)
            v = v.rearrange(pat, **{nm: int(s) for nm, s in zip(names[1:], free_shape[1:])})
        return v


def gblock(c, b):
    i = b // 2
    return 16 * i + (c if b % 2 == 0 else 15 - c)


def build(stop_after=None, debug=False):
    nc = bass.Bass("TRN2", target_bir_lowering=False)
    okind = "ExternalOutput" if debug else "Internal"

    def din(name, shape, dt=F32):
        return nc.dram_tensor(name, list(shape), dt, kind="ExternalInput").ap()

    def dscr(name, shape, dt, kind="Internal"):
        return nc.dram_tensor(name, list(shape), dt, kind=kind).ap()

    xin = din("xin", [T0, D])
    mem = din("mem", [256, D])
    w_in = din("w_in", [DEPTH, D, 4096])
    w_out = din("w_out", [DEPTH, D, D])
    w_mkv = din("w_mem_kv", [DEPTH, D, 1024])
    ln_g = din("ln_g", [DEPTH, D])
    ln_b = din("ln_b", [DEPTH, D])
    pool_w = din("pool_w", [2, 4, 384, 384])
    w_kv = din("w_kv_shared", [D, 3072])
    t_pscale = din("t_pscale", [128, 24])
    t_hmask = din("t_hmask", [128, BH])
    t_invcnt = din("t_invcnt", [128, 4 * BH])
    t_pm = din("t_pm", [128, 16 * NSLOT])
    t_p01 = din("t_p01", [128, 16 * NSLOT])
    t_tri = din("t_tri", [128, 512])
    out = nc.dram_tensor("out", [T1, D], F32, kind="ExternalOutput").ap()

    hT_scr = dscr("hT_scr", [128, 16, T0], BF16)
    mixT_scr = dscr("mixT_scr", [128, 16, T0], BF16)
    h_a = dscr("h_a", [T0, D], F32, okind)
    h_b = dscr("h_b", [T1, D], F32, okind)
    QT_scr = dscr("QT_scr", [128, NH, T1], BF16)
    SG_scr = dscr("SG_scr", [128, NH, T1], BF16)
    kT_loc = dscr("kT_loc", [NH * 128, T1], BF16)
    V_loc = dscr("V_loc", [NH * NBLK * 128, 260], BF16)
    kmT_loc = dscr("kmT_loc", [128, NH * NBLK], BF16)
    kT_all = dscr("kT_all", [NCORES * NH * 128, T1], BF16)
    V_all = dscr("V_all", [NCORES * NH * NBLK * 128, 260], BF16)
    kmT_all = dscr("kmT_all", [NCORES * 128, NH * NBLK], BF16)

    st = contextlib.ExitStack()
    with st:
        ARENA_BYTES = 204 * 1024
        arena_t = st.enter_context(nc.sbuf_tensor("arena", [128, ARENA_BYTES], U8))
        AR = Arena(arena_t, ARENA_BYTES)
        pbig = [st.enter_context(nc.psum_tensor(f"pbig{i}", [128, 1024], F32)) for i in range(4)]
        psb = [pbig[i // 2][:, (i % 2) * 512:(i % 2 + 1) * 512] for i in range(8)]
        P = Prog(nc)
        ps_ctr = [0]

        def getps():
            i = ps_ctr[0] % 8
            ps_ctr[0] += 1
            return i

        def dma(eng, out_, in_, reads=(), writes=()):
            P.op(eng, lambda e: e.dma_start(out=out_, in_=in_), reads, writes, dma=True)

        def mm(out_, lhsT, rhs, start, stop, reads, writes):
            P.op("pe", lambda e: e.matmul(out_, lhsT=lhsT, rhs=rhs, start=start, stop=stop), reads, writes)

        def act(out_, in_, func, reads, writes, scale=None):
            if scale is None:
                P.op("act", lambda e: e.activation(out=out_, in_=in_, func=func), reads, writes)
            else:
                P.op("act", lambda e: e.activation(out=out_, in_=in_, func=func, scale=scale), reads, writes)

        def cp(eng, out_, in_, reads, writes):
            if eng == "act":
                act(out_, in_, AF.Copy, reads, writes)
            else:
                P.op(eng, lambda e: e.tensor_copy(out=out_, in_=in_), reads, writes)

        def tt(eng, out_, in0, in1, op, reads, writes):
            P.op(eng, lambda e: e.tensor_tensor(out=out_, in0=in0, in1=in1, op=op), reads, writes)

        def ts(eng, out_, in0, s1, s2, op0, op1, reads, writes):
            if op1 is None:
                P.op(eng, lambda e: e.tensor_scalar(out=out_, in0=in0, scalar1=s1, scalar2=None, op0=op0), reads, writes)
            else:
                P.op(eng, lambda e: e.tensor_scalar(out=out_, in0=in0, scalar1=s1, scalar2=s2, op0=op0, op1=op1),
                     reads, writes)

        def stt(eng, out_, in0, scalar, in1, op0, op1, reads, writes):
            P.op(eng, lambda e: e.scalar_tensor_tensor(out=out_, in0=in0, scalar=scalar, in1=in1, op0=op0, op1=op1),
                 reads, writes)

        def memset(eng, ap, val, writes):
            P.op(eng, lambda e: e.memset(ap, val), (), writes)

        ident = AR.alloc([128], BF16)
        identf = AR.alloc([128], F32)
        ones = AR.alloc([128], BF16)
        memT = AR.alloc([16, 256], BF16)
        hmask = AR.alloc([BH], F32)
        invcnt = AR.alloc([4, BH], F32)
        pm16 = AR.alloc([16, NSLOT], F32)
        p01 = AR.alloc([16, NSLOT], F32)
        tri = AR.alloc([2, 256], BF16)
        pscale = AR.alloc([24], F32)
        kmT = AR.alloc([NCORES, NH * NBLK], BF16)
        PERSIST_END = AR.off

        memset("pool", identf, 1.0, ["identf"])
        P.op("pool", lambda e: e.affine_select(out=identf, in_=identf, pattern=[[-1, 128]], compare_op=ALU.is_equal,
                                              fill=0.0, base=0, channel_multiplier=1), ["identf"], ["identf"])
        cp("dve", ident, identf, ["identf"], ["ident"])
        memset("pool", ones, 1.0, ["ones"])
        dma("sp", hmask, t_hmask, (), ["hmask"])
        dma("sp", invcnt, t_invcnt.rearrange("p (g t) -> p g t", t=BH), (), ["invcnt"])
        dma("sp", pm16, t_pm.rearrange("p (a s) -> p a s", s=NSLOT), (), ["pm16"])
        dma("sp", p01, t_p01.rearrange("p (a s) -> p a s", s=NSLOT), (), ["p01"])
        dma("pool", tri, t_tri.rearrange("p (c q) -> p c q", q=256), (), ["tri"])
        dma("sp", pscale, t_pscale, (), ["pscale"])

        def transpose_store(src_bf, src_keys, hTt, hTt_key, tok, evac_engs=("act", "dve")):
            for half in range(2):
                pi = getps()
                pv = psb[pi][:].bitcast(BF16).rearrange("p (a b) -> p a b", b=128)
                for j in range(8):
                    dc = half * 8 + j
                    P.op("pe", lambda e, o_=pv[:, j, :], i_=src_bf[:, dc * 128:(dc + 1) * 128]: e.transpose(o_, i_, ident),
                         list(src_keys) + ["ident"], [f"ps{pi}"])
                cp(evac_engs[half], hTt[:, half * 8:(half + 1) * 8, :], pv, [f"ps{pi}"], [hTt_key + str(half)])
            dma("sp", hT_scr[:, :, tok:tok + 128], hTt, [hTt_key + "0", hTt_key + "1"], [("hT_scr", tok)])

        AR.off = PERSIST_END
        x32 = [AR.alloc([D], F32) for _ in range(2)]
        xbf = [AR.alloc([D], BF16) for _ in range(2)]
        hTt0 = [AR.alloc([16, 128], BF16) for _ in range(2)]
        for mt_ in range(2):
            s = mt_
            dma("sp", x32[s], mem[mt_ * 128:(mt_ + 1) * 128, :], (), [f"x32_{s}"])
            cp("dve", xbf[s][:, 0:1024], x32[s][:, 0:1024], [f"x32_{s}"], [f"xbfa{s}"])
            cp("pool", xbf[s][:, 1024:2048], x32[s][:, 1024:2048], [f"x32_{s}"], [f"xbfb{s}"])
            for half in range(2):
                pi = getps()
                pv = psb[pi][:].bitcast(BF16).rearrange("p (a b) -> p a b", b=128)
                for j in range(8):
                    dc = half * 8 + j
                    P.op("pe", lambda e, o_=pv[:, j, :], i_=xbf[s][:, dc * 128:(dc + 1) * 128]: e.transpose(o_, i_, ident),
                         [f"xbfa{s}", f"xbfb{s}", "ident"], [f"ps{pi}"])
                cp("act", memT[:, half * 8:(half + 1) * 8, mt_ * 128:(mt_ + 1) * 128], pv, [f"ps{pi}"], ["memT"])
        dma("sp", x32[0], xin[0:128, :], (), ["x32_0"])
        for t in range(T0 // 128):
            s = t % 2
            if t + 1 < T0 // 128:
                dma("sp", x32[1 - s], xin[(t + 1) * 128:(t + 2) * 128, :], (), [f"x32_{1 - s}"])
            cp("dve", xbf[s][:, 0:1024], x32[s][:, 0:1024], [f"x32_{s}"], [f"xbfa{s}"])
            cp("pool", xbf[s][:, 1024:2048], x32[s][:, 1024:2048], [f"x32_{s}"], [f"xbfb{s}"])
            transpose_store(xbf[s], [f"xbfa{s}", f"xbfb{s}"], hTt0[s], f"hTt{s}_", t * 128)
        P.barrier()

        def stage1(L):
            pool_layer = L < 2
            Tin = T0 if pool_layer else T1
            N = BH if pool_layer else 512
            nblk = Tin // N
            AR.off = PERSIST_END
            hT = AR.alloc([16, Tin], BF16)
            WU = 768
            wu = [AR.alloc([16, WU], BF16) for _ in range(2)]
            pw = [AR.alloc([3, 384], BF16) for _ in range(2)]
            mkT = AR.alloc([4, 256], BF16)
            mv = AR.alloc([2, 512], BF16)
            for q in range(4):
                dma("sp", hT[:, 4 * q:4 * q + 4, :], hT_scr[:, 4 * q:4 * q + 4, 0:Tin], (), [f"hT{q}"])
            wctr = [0]
            specs = [(w_mkv[L], [(0, 512)], None), (w_mkv[L], [(512, 512)], None)]
            if L == 2:
                specs += [(w_kv, [(512 * u, 512)], None) for u in range(3)]
                specs += [(w_kv, [(1536 + 512 * u, 512)], None) for u in range(3)]
            if pool_layer:
                specs += [(w_in[L], [(384 * g, 384), (1536 + 384 * g, 384)], g) for g in range(4)]
            else:
                specs += [(w_in[L], [(512 * u, 512)], None) for u in range(3)]
                specs += [(w_in[L], [(1536 + 512 * u, 512)], None) for u in range(3)]
            specs += [(w_in[L], [(3072 + 256 * m, 256), (3584 + 256 * m, 256)], None) for m in range(2)]

            def issue_w(k):
                w2d, col_slices, g = specs[k]
                bi = k % 2
                off = 0
                for (c0, n) in col_slices:
                    dma("pool", wu[bi][:, :, off:off + n], w2d[:, c0:c0 + n].rearrange("(dc p) n -> p dc n", p=128),
                        (), [f"wu{bi}"])
                    off += n
                if g is not None:
                    dma("pool", pw[bi], pool_w[L, g].rearrange("(c p) d -> p c d", p=128), (), [f"pw{bi}"])

            def load_w(w2d, col_slices):
                k = wctr[0]
                assert specs[k][1] == col_slices, (k, specs[k][1], col_slices)
                if k == 0:
                    issue_w(0)
                if k + 1 < len(specs):
                    issue_w(k + 1)
                wctr[0] += 1
                return k % 2

            def proj_chunk(bi, widx, tok0, n):
                pi = getps()
                for dc in range(16):
                    mm(psb[pi][:, 0:n], wu[bi][:, dc, widx * 128:(widx + 1) * 128], hT[:, dc, tok0:tok0 + n],
                       dc == 0, dc == 15, [f"wu{bi}", f"hT{dc // 4}"], [f"ps{pi}"])
                return pi

            bi = load_w(w_mkv[L], [(0, 512)])
            for h in range(4):
                pi = getps()
                for dc in range(16):
                    mm(psb[pi][:, 0:256], wu[bi][:, dc, h * 128:(h + 1) * 128], memT[:, dc, :], dc == 0, dc == 15,
                       [f"wu{bi}", "memT"], [f"ps{pi}"])
                cp("act", mkT[:, h, :], psb[pi][:, 0:256], [f"ps{pi}"], ["mkT"])
            bi = load_w(w_mkv[L], [(512, 512)])
            for mc in range(2):
                pi = getps()
                for dc in range(16):
                    mm(psb[pi][:, 0:512], memT[:, dc, mc * 128:(mc + 1) * 128], wu[bi][:, dc, 0:512], dc == 0, dc == 15,
                       [f"wu{bi}", "memT"], [f"ps{pi}"])
                cp("act", mv[:, mc, :], psb[pi][:, 0:512], [f"ps{pi}"], ["mv"])

            if L == 2:
                kst = [AR.alloc([2, 2, 128], BF16) for _ in range(3)]
                km32 = AR.alloc([NH, NBLK], F32)
                kmb = AR.alloc([NH * NBLK], BF16)
                memset("pool", km32, 0.0, ["km32z"])
                vst = [AR.alloc([4, 2, 130], BF16) for _ in range(2)]
                for s in range(2):
                    memset("pool", vst[s][:, :, :, 128:129], 1.0, [f"vst{s}"])
                    memset("pool", vst[s][:, :, :, 129:130], 0.0, [f"vst{s}"])
                kctr = 0
                for u in range(0 if "SKIPK" not in os.environ else 3, 3):
                    bi = load_w(w_kv, [(512 * u, 512)])
                    for blk in range(4):
                        tok0 = blk * 512
                        for hh in range(4):
                            h = 4 * u + hh
                            pi = proj_chunk(bi, hh, tok0, 512)
                            ks = kctr % 3
                            kctr += 1
                            for b2 in range(2):
                                src_ = psb[pi][:, b2 * 256:(b2 + 1) * 256].rearrange("d (p c) -> d c p", c=2)
                                P.op("act", lambda e, o_=kst[ks][:, b2, :, :], i_=src_,
                                     a_=km32[:, h, 2 * blk + b2:2 * blk + b2 + 1]:
                                     e.activation(out=o_, in_=i_, func=AF.Copy, accum_out=a_),
                                     [f"ps{pi}", "km32z"], [f"kst{ks}", ("km32", h, 2 * blk + b2)])
                            dma("sp", kT_loc[h * 128:(h + 1) * 128, tok0:tok0 + 512],
                                kst[ks].rearrange("d b c p -> d (b c p)"), [f"kst{ks}"], ["kT_loc"])
                ts("dve", kmb, km32.rearrange("p h b -> p (h b)"), 1.0 / 256.0, None, ALU.mult, None,
                   [("km32", h_, b_) for h_ in range(NH) for b_ in range(NBLK)], ["kmb"])
                dma("sp", kmT_loc, kmb, ["kmb"], ["kmT_loc"])
                vctr = 0
                for u in range(0 if "SKIPV" not in os.environ else 3, 3):
                    bi = load_w(w_kv, [(1536 + 512 * u, 512)])
                    for b in range(NBLK):
                        vs = vctr % 2
                        vctr += 1
                        for c in range(2):
                            pi = getps()
                            for dc in range(16):
                                mm(psb[pi][:, 0:512], hT[:, dc, b * 256 + c:(b + 1) * 256:2], wu[bi][:, dc, 0:512],
                                   dc == 0, dc == 15, [f"wu{bi}", f"hT{dc // 4}"], [f"ps{pi}"])
                            cp("act" if c == 0 else "dve", vst[vs][:, :, c, 0:128],
                               psb[pi][:, 0:512].rearrange("p (h d) -> p h d", d=128), [f"ps{pi}"], [f"vst{vs}"])
                        for hh in range(4):
                            h = 4 * u + hh
                            r0 = (h * NBLK + b) * 128
                            dma("sp", V_loc[r0:r0 + 128, :], vst[vs][:, hh, :, :].rearrange("p c x -> p (c x)"),
                                [f"vst{vs}"], ["V_loc"])
                grp = [list(range(NCORES))]
                if "NOAG" in os.environ:
                    grp = None
                if grp is not None:
                  P.op("pool", lambda e: e.collective_compute("AllGather", ALU.bypass, replica_groups=grp,
                                                            ins=[kT_loc.opt()], outs=[kT_all.opt()]),
                     ["kT_loc"], ["kT_all"], cc=True)
                if grp is not None:
                  P.op("pool", lambda e: e.collective_compute("AllGather", ALU.bypass, replica_groups=grp,
                                                            ins=[V_loc.opt()], outs=[V_all.opt()]),
                     ["V_loc"], ["V_all"], cc=True)
                if grp is not None:
                  P.op("pool", lambda e: e.collective_compute("AllGather", ALU.bypass, replica_groups=grp,
                                                            ins=[kmT_loc.opt()], outs=[kmT_all.opt()]),
                     ["kmT_loc"], ["kmT_all"], cc=True)

            if pool_layer:
                uU = [AR.alloc([3, BH], F32) for _ in range(2)]
                uA = [AR.alloc([3, BH], F32) for _ in range(2)]
                uB = [AR.alloc([3, BH], F32) for _ in range(2)]
                pl = [AR.alloc([3, BH], BF16) for _ in range(2)]
                sg = [AR.alloc([3, BH], BF16) for _ in range(2)]
                mx = [AR.alloc([3, BH], BF16) for _ in range(2)]
                tmpc = AR.alloc([BH], F32)
                for s in range(2):
                    memset("pool", uA[s], 0.0, [f"uA{s}"])
                    memset("pool", uB[s], 0.0, [f"uB{s}"])
            else:
                qst = [AR.alloc([512], BF16) for _ in range(4)]
            qm = [AR.alloc([N], BF16) for _ in range(2)]
            pTm = [AR.alloc([2, N], BF16) for _ in range(2)]
            sgm = [AR.alloc([N], F32) for _ in range(2)]
            rdm = [AR.alloc([N], F32) for _ in range(2)]
            t1m = [AR.alloc([N], F32) for _ in range(2)]
            mxm = [AR.alloc([N], BF16) for _ in range(2)]

            if pool_layer:
                cnt = 0
                for g in range(4):
                    w = POOL_W[g]
                    bi = load_w(w_in[L], [(384 * g, 384), (1536 + 384 * g, 384)])
                    for b in range(NBLK):
                        s = cnt % 2
                        cnt += 1
                        tok0 = b * BH
                        U, A, B = uU[s], uA[s], uB[s]
                        for c in range(3):
                            pi = proj_chunk(bi, c, tok0, BH)
                            if b == 0:
                                tt("dve", U[:, c, :], psb[pi][:, 0:BH], hmask, ALU.mult, [f"ps{pi}", "hmask"], [f"uU{s}"])
                            else:
                                cp("act", U[:, c, :], psb[pi][:, 0:BH], [f"ps{pi}"], [f"uU{s}"])
                        tt("dve", A[:, :, 1:], U[:, :, 1:], U[:, :, 0:BH - 1], ALU.add, [f"uU{s}"], [f"uA{s}"])
                        S_, Sk = A, f"uA{s}"
                        if w >= 4:
                            tt("pool", B[:, :, 3:], A[:, :, 3:], A[:, :, 1:BH - 2], ALU.add, [f"uA{s}"], [f"uB{s}"])
                            S_, Sk = B, f"uB{s}"
                        if w >= 8:
                            tt("dve", A[:, :, 7:], B[:, :, 7:], B[:, :, 3:BH - 4], ALU.add, [f"uB{s}"], [f"uA{s}"])
                            S_, Sk = A, f"uA{s}"
                        if w >= 16:
                            tt("pool", B[:, :, 15:], A[:, :, 15:], A[:, :, 7:BH - 8], ALU.add, [f"uA{s}"], [f"uB{s}"])
                            S_, Sk = B, f"uB{s}"
                        if b == 0:
                            for c in range(3):
                                tt("dve", tmpc, S_[:, c, :], invcnt[:, g, :], ALU.mult, [Sk, "invcnt"], ["tmpc"])
                                tt("dve", pl[s][:, c, :], tmpc, U[:, c, :], ALU.subtract, ["tmpc", f"uU{s}"], [f"pl{s}"])
                        else:
                            stt("dve", pl[s], S_, 1.0 / w, U, ALU.mult, ALU.subtract, [Sk, f"uU{s}"], [f"pl{s}"])
                        for j in range(3):
                            pi = proj_chunk(bi, 3 + j, tok0, BH)
                            act(sg[s][:, j, :], psb[pi][:, 0:BH], AF.Silu, [f"ps{pi}"], [f"sg{s}"])
                        pjs = []
                        for j in range(3):
                            pj = getps()
                            pjs.append(pj)
                            for c in range(3):
                                mm(psb[pj][:, 0:BH], pw[bi][:, c, j * 128:(j + 1) * 128], pl[s][:, c, :], c == 0, c == 2,
                                   [f"pw{bi}", f"pl{s}"], [f"ps{pj}"])
                        for j in range(3):
                            k = L * 12 + 3 * g + j
                            stt("dve", mx[s][:, j, :], psb[pjs[j]][:, 0:BH], pscale[:, k:k + 1], sg[s][:, j, :], ALU.mult,
                                ALU.mult, [f"ps{pjs[j]}", "pscale", f"sg{s}"], [f"mx{s}"])
                        dma("sp", mixT_scr[:, 3 * g:3 * g + 3, tok0:tok0 + BH], mx[s], [f"mx{s}"], [("mixT", g, b)])
            else:
                qc = 0
                for u in range(0 if "SKIPQ" not in os.environ else 3, 3):
                    bi = load_w(w_in[L], [(512 * u, 512)])
                    for blk in range(4):
                        tok0 = blk * 512
                        for hh in range(4):
                            h = 4 * u + hh
                            pi = proj_chunk(bi, hh, tok0, 512)
                            s = qc % 4
                            qc += 1
                            cp("act" if qc % 2 else "dve", qst[s], psb[pi][:, 0:512], [f"ps{pi}"], [f"qst{s}"])
                            dma("sp", QT_scr[:, h, tok0:tok0 + 512], qst[s], [f"qst{s}"], [("QT", h, blk)])
                for u in range(3):
                    bi = load_w(w_in[L], [(1536 + 512 * u, 512)])
                    for blk in range(4):
                        tok0 = blk * 512
                        for hh in range(4):
                            h = 4 * u + hh
                            pi = proj_chunk(bi, hh, tok0, 512)
                            s = qc % 4
                            qc += 1
                            act(qst[s], psb[pi][:, 0:512], AF.Silu, [f"ps{pi}"], [f"qst{s}"])
                            dma("sp", SG_scr[:, h, tok0:tok0 + 512], qst[s], [f"qst{s}"], [("SG", h, blk)])

            cnt = 0
            for m in range(0 if "SKIPM" not in os.environ else 2, 2):
                bi = load_w(w_in[L], [(3072 + 256 * m, 256), (3584 + 256 * m, 256)])
                for blk in range(nblk):
                    tok0 = blk * N
                    for hh in range(2):
                        h = 2 * m + hh
                        s = cnt % 2
                        cnt += 1
                        pi = proj_chunk(bi, hh, tok0, N)
                        cp("dve", qm[s][:, 0:N], psb[pi][:, 0:N], [f"ps{pi}"], [f"qm{s}"])
                        for mc in range(2):
                            p_s = getps()
                            mm(psb[p_s][:, 0:N], mkT[:, h, mc * 128:(mc + 1) * 128], qm[s][:, 0:N], True, True,
                               ["mkT", f"qm{s}"], [f"ps{p_s}"])
                            act(pTm[s][:, mc, 0:N], psb[p_s][:, 0:N], AF.Exp, [f"ps{p_s}"], [f"pTm{s}"], scale=SCALE)
                        p_o = getps()
                        for mc in range(2):
                            mm(psb[p_o][:, 0:N], mv[:, mc, h * 128:(h + 1) * 128], pTm[s][:, mc, 0:N], mc == 0, mc == 1,
                               ["mv", f"pTm{s}"], [f"ps{p_o}"])
                        p_d = getps()
                        for mc in range(2):
                            mm(psb[p_d][:, 0:N], ones, pTm[s][:, mc, 0:N], mc == 0, mc == 1, ["ones", f"pTm{s}"],
                               [f"ps{p_d}"])
                        p_g = proj_chunk(bi, 2 + hh, tok0, N)
                        act(sgm[s][:, 0:N], psb[p_g][:, 0:N], AF.Silu, [f"ps{p_g}"], [f"sgm{s}"])
                        P.op("dve", lambda e, o_=rdm[s][:, 0:N], i_=psb[p_d][:, 0:N]: e.reciprocal(out=o_, in_=i_),
                             [f"ps{p_d}"], [f"rdm{s}"])
                        tt("dve", t1m[s][:, 0:N], psb[p_o][:, 0:N], rdm[s][:, 0:N], ALU.mult, [f"ps{p_o}", f"rdm{s}"],
                           [f"t1m{s}"])
                        tt("pool", mxm[s][:, 0:N], t1m[s][:, 0:N], sgm[s][:, 0:N], ALU.mult, [f"t1m{s}", f"sgm{s}"],
                           [f"mxm{s}"])
                        dma("sp", mixT_scr[:, 12 + h, tok0:tok0 + N], mxm[s][:, 0:N], [f"mxm{s}"], [("mixT", 12 + h, blk)])
            P.barrier()

        def stage2(L):
            AR.off = PERSIST_END
            kT = AR.alloc([NCORES, NBLK, 256], BF16)
            vv = AR.alloc([NCORES, NBLK, 260], BF16)
            kTo = [AR.alloc([NBLK, 256], BF16) for _ in range(2)]
            vo = [AR.alloc([NBLK, 260], BF16) for _ in range(2)]
            qT = [AR.alloc([T1], BF16) for _ in range(2)]
            sgT = [AR.alloc([T1], BF16) for _ in range(2)]
            mxh = [AR.alloc([T1], BF16) for _ in range(2)]
            sel = [AR.alloc([16, NSLOT], F32) for _ in range(2)]
            mg = AR.alloc([8, NSLOT], F32)
            top8 = AR.alloc([16, 8], F32)
            acc = [AR.alloc([4, 129], F32) for _ in range(2)]
            pT = [AR.alloc([2, 512], BF16) for _ in range(3)]
            pTo = [AR.alloc([2, 256], BF16) for _ in range(2)]
            rc = AR.alloc([16], F32)
            obf = [AR.alloc([128], BF16) for _ in range(2)]
            cnt = {"S": 0, "O": 0, "pT": 0, "pTo": 0, "obf": 0}
            if L == 2:
                dma("sp", kmT, kmT_all.rearrange("(r d) x -> d r x", d=128), (), ["kmT"])

            def nxt(name, n):
                v = cnt[name] % n
                cnt[name] += 1
                return v

            def Oview(k):
                return pbig[2 + k][:].rearrange("p (bk x) -> p bk x", x=512)[:, :, 0:258].rearrange(
                    "p bk (s x) -> p bk s x", x=129)

            def small_loads(h_):
                hs_ = h_ % 2
                dma("sp", kTo[hs_], kT_loc[h_ * 128:(h_ + 1) * 128, :].rearrange("d (b x) -> d b x", x=256), ["kT_loc"],
                    [f"kTo{hs_}"])
                dma("sp", vo[hs_], V_loc.rearrange("(hh b p) x -> hh p b x", hh=NH, b=NBLK)[h_], ["V_loc"], [f"vo{hs_}"])
                dma("sp", qT[hs_], QT_scr[:, h_, :], [("QT", h_, k) for k in range(4)], [f"qT{hs_}"])
                dma("sp", sgT[hs_], SG_scr[:, h_, :], [("SG", h_, k) for k in range(4)], [f"sgT{hs_}"])

            def kv_group_load(h_, i):
                src = kT_all.rearrange("(r hd) (b x) -> hd r b x", r=NCORES, x=256)[h_ * 128:(h_ + 1) * 128, :, 2 * i:2 * i + 2, :]
                dma("sp", kT[:, :, 2 * i:2 * i + 2, :], src, ["kT_all"], [f"kT_g{i}"])
                srcv = V_all.rearrange("(r hh b p) x -> hh p r b x", r=NCORES, hh=NH, b=NBLK)[h_, :, :, 2 * i:2 * i + 2, :]
                for b2 in range(2):
                    dma("sp", vv[:, :, 2 * i + b2, :], srcv[:, :, b2, :], ["V_all"], [f"v_g{i}"])

            for h in range(NH):
                hs = h % 2
                if h == 0:
                    small_loads(0)
                    for i in range(3, -1, -1):
                        kv_group_load(0, i)
                kmh = kmT.rearrange("d r (hh b) -> d hh r b", b=NBLK)[:, h, :, :]
                for half in range(2):
                    sk = nxt("S", 2)
                    pv = pbig[sk][:, 0:512].rearrange("p (a s) -> p a s", s=NSLOT)
                    for t8 in range(8):
                        t = half * 8 + t8
                        mm(pv[:, t8, :].rearrange("p (r b) -> p r b", b=NBLK), qT[hs][:, t * 128:(t + 1) * 128], kmh,
                           True, True, [f"qT{hs}", "kmT"], [f"S{sk}"])
                    tt("dve", mg, pv, pm16[:, half * 8:(half + 1) * 8, :], ALU.add, [f"S{sk}", "pm16"], ["mg"])
                    for t8 in range(8):
                        t = half * 8 + t8
                        P.op("dve", lambda e, o_=top8[:, t, :], i_=mg[:, t8, :]: e.max(out=o_, in_=i_), ["mg"], [("top8", t)])
                        stt("dve", sel[hs][:, t, :], mg[:, t8, :], top8[:, t, 2:3], p01[:, t, :], ALU.is_ge, ALU.mult,
                            ["mg", ("top8", t), "p01"], [f"sel{hs}"])
                if h + 1 < NH:
                    small_loads(h + 1)
                items = []
                for i in range(3, -1, -1):
                    items.append({"k": "own", "i": i, "X": 0})
                    items.append({"k": "own", "i": i, "X": 1})
                    for r in range(NCORES):
                        for b in range(2 * i + 2):
                            items.append({"k": "g", "i": i, "r": r, "b": b})
                    items[-1]["last"] = True

                def front(it):
                    i = it["i"]
                    sk = nxt("S", 2)
                    if it["k"] == "own":
                        b = 2 * i + it["X"]
                        for c in range(2):
                            mm(pbig[sk][:, c * 256:(c + 1) * 256], kTo[hs][:, b, c * 128:(c + 1) * 128],
                               qT[hs][:, b * 256:(b + 1) * 256], True, True, [f"kTo{hs}", f"qT{hs}"], [f"S{sk}"])
                        os_ = nxt("pTo", 2)
                        po = pTo[os_]
                        act(po, pbig[sk][:, 0:512].rearrange("p (c q) -> p c q", q=256), AF.Exp, [f"S{sk}"],
                            [f"pTo{os_}"], scale=SCALE)
                        tt("pool", po, po, tri, ALU.mult, [f"pTo{os_}", "tri"], [f"pTo{os_}"])
                        it["p"] = os_
                    else:
                        r, b = it["r"], it["b"]
                        g_ = b // 2
                        qpair = qT[hs][:, 512 * i:512 * i + 512]
                        for c in range(2):
                            mm(pbig[sk][:, c * 512:(c + 1) * 512], kT[:, r, b, c * 128:(c + 1) * 128], qpair, True, True,
                               [f"kT_g{g_}", f"qT{hs}"], [f"S{sk}"])
                        ps_ = nxt("pT", 3)
                        act(pT[ps_], pbig[sk][:].rearrange("p (c q) -> p c q", q=512), AF.Exp, [f"S{sk}"], [f"pT{ps_}"],
                            scale=SCALE)
                        it["p"] = ps_

                def back(it):
                    i = it["i"]
                    a_s = i % 2
                    A_ = acc[a_s]
                    ak = f"acc{a_s}"
                    ok = nxt("O", 2)
                    O = Oview(ok)
                    if it["k"] == "own":
                        X = it["X"]
                        b = 2 * i + X
                        po = pTo[it["p"]]
                        for s2 in range(2):
                            for c in range(2):
                                mm(O[:, 0, s2, :], po[:, c, s2 * 128:(s2 + 1) * 128], vo[hs][:, b, c * 130:c * 130 + 129],
                                   c == 0, c == 1, [f"pTo{it['p']}", f"vo{hs}"], [f"O{ok}"])
                        cp("act", A_[:, 2 * X:2 * X + 2, :], O[:, 0, :, :], [f"O{ok}"],
                           [f"{ak}_{2 * X}", f"{ak}_{2 * X + 1}"])
                    else:
                        r, b = it["r"], it["b"]
                        g_ = b // 2
                        slot = r * NBLK + b
                        pt = pT[it["p"]]
                        for qt in range(4):
                            for c in range(2):
                                mm(O[:, qt // 2, qt % 2, :], pt[:, c, qt * 128:(qt + 1) * 128],
                                   vv[:, r, b, c * 130:c * 130 + 129], c == 0, c == 1,
                                   [f"pT{it['p']}", f"v_g{g_}"], [f"O{ok}"])
                        for qt in range(4):
                            t = 4 * i + qt
                            stt("dve", A_[:, qt, :], O[:, qt // 2, qt % 2, :], sel[hs][:, t, slot:slot + 1], A_[:, qt, :],
                                ALU.mult, ALU.add, [f"O{ok}", f"sel{hs}", f"{ak}_{qt}"], [f"{ak}_{qt}"])
                    if it.get("last") and h + 1 < NH:
                        kv_group_load(h + 1, i)
                    if it.get("last"):
                        for qt in range(4):
                            t = 4 * i + qt
                            P.op("dve", lambda e, o_=rc[:, t:t + 1], i_=A_[:, qt, 128:129]: e.reciprocal(out=o_, in_=i_),
                                 [f"{ak}_{qt}"], [("rc", t)])
                            os_ = nxt("obf", 2)
                            ts("dve", obf[os_], A_[:, qt, 0:128], rc[:, t:t + 1], None, ALU.mult, None,
                               [f"{ak}_{qt}", ("rc", t)], [f"obf{os_}"])
                            ok2 = nxt("O", 2)
                            ptv = pbig[2 + ok2][:, 0:512].bitcast(BF16)[:, 0:128]
                            P.op("pe", lambda e, o_=ptv, i_=obf[os_]: e.transpose(o_, i_, ident), [f"obf{os_}", "ident"],
                                 [f"O{ok2}"])
                            tt("dve", mxh[hs][:, t * 128:(t + 1) * 128], ptv, sgT[hs][:, t * 128:(t + 1) * 128], ALU.mult,
                               [f"O{ok2}", f"sgT{hs}"], [f"mxh{hs}"])

                for idx in range(len(items) + 1):
                    if idx < len(items):
                        front(items[idx])
                    if idx >= 1:
                        back(items[idx - 1])
                dma("sp", mixT_scr[:, h, 0:T1], mxh[hs], [f"mxh{hs}"], [("mixT", h)])
            P.barrier()

        def stage3(L, hsrc, hdst, tiles, last):
            AR.off = PERSIST_END
            wo = AR.alloc([16, D], BF16)
            gb = AR.alloc([2, D], F32)
            mt = [AR.alloc([16, 128], BF16) for _ in range(2)]
            h32 = [AR.alloc([D], F32) for _ in range(2)]
            z = [AR.alloc([D], F32) for _ in range(2)]
            hn = [AR.alloc([D], F32) for _ in range(2)]
            hb = [AR.alloc([D], BF16) for _ in range(2)]
            hTt = [AR.alloc([16, 128], BF16) for _ in range(2)]
            bst = [AR.alloc([4, 6], F32) for _ in range(2)]
            mvr = [AR.alloc([4], F32) for _ in range(2)]
            for q in range(4):
                dma("pool", wo[:, 4 * q:4 * q + 4, :],
                    w_out[L, 512 * q:512 * (q + 1), :].rearrange("(fc p) n -> p fc n", p=128), (), [f"wo{q}"])
            dma("sp", gb[:, 0, :], ln_g[L:L + 1, :].partition_broadcast(128), (), ["gb0"])
            dma("sp", gb[:, 1, :], ln_b[L:L + 1, :].partition_broadcast(128), (), ["gb1"])
            def tile_loads(ti_):
                s_ = ti_ % 2
                src_ = tiles[ti_][0]
                dma("sp", mt[s_], mixT_scr[:, :, src_:src_ + 128], (), [f"mt{s_}"])
                dma("sp", h32[s_], hsrc[src_:src_ + 128, :], (), [f"h32{s_}"])

            def partA(ti):
                s = ti % 2
                for og in range(4):
                    pi = getps()
                    for fc in range(16):
                        mm(psb[pi][:, 0:512], mt[s][:, fc, :], wo[:, fc, og * 512:(og + 1) * 512], fc == 0, fc == 15,
                           [f"mt{s}", f"wo{fc // 4}"], [f"ps{pi}"])
                    stt("dve", z[s][:, og * 512:(og + 1) * 512], h32[s][:, og * 512:(og + 1) * 512], ALPHA,
                        psb[pi][:, 0:512], ALU.mult, ALU.add, [f"h32{s}", f"ps{pi}"], [f"z{s}_{og}"])
                    P.op("dve", lambda e, o_=bst[s][:, og, :], i_=z[s][:, og * 512:(og + 1) * 512]: e.bn_stats(out=o_, in_=i_),
                         [f"z{s}_{og}"], [f"bst{s}"])

            def partB(ti):
                s = ti % 2
                dst = tiles[ti][1]
                P.op("dve", lambda e, o_=mvr[s][:, 0:2], i_=bst[s].rearrange("p a b -> p (a b)"): e.bn_aggr(out=o_, in_=i_),
                     [f"bst{s}"], [f"mvr{s}"])
                ts("dve", mvr[s][:, 3:4], mvr[s][:, 1:2], LN_EPS, None, ALU.add, None, [f"mvr{s}"], [f"mve{s}"])
                act(mvr[s][:, 2:3], mvr[s][:, 3:4], AF.Sqrt, [f"mve{s}"], [f"msd{s}"])
                P.op("dve", lambda e, o_=mvr[s][:, 2:3], i_=mvr[s][:, 2:3]: e.reciprocal(out=o_, in_=i_), [f"msd{s}"], [f"mrs{s}"])
                zk = [f"z{s}_{og}" for og in range(4)]
                ts("dve", z[s], z[s], mvr[s][:, 0:1], mvr[s][:, 2:3], ALU.subtract, ALU.mult, zk + [f"mvr{s}", f"mrs{s}"], zk)
                tt("pool", hn[s], z[s], gb[:, 0, :], ALU.mult, zk + ["gb0"], [f"hn{s}"])
                tt("pool", hn[s], hn[s], gb[:, 1, :], ALU.add, [f"hn{s}", "gb1"], [f"hn{s}"])
                dma("sp", hdst[dst:dst + 128, :], hn[s], [f"hn{s}"], [("hdst", dst)])
                if not last:
                    cp("act", hb[s], hn[s], [f"hn{s}"], [f"hb{s}"])

            def partC(ti):
                s = ti % 2
                dst = tiles[ti][1]
                if not last:
                    transpose_store(hb[s], [f"hb{s}"], hTt[s], f"hTt{s}_", dst, evac_engs=("act", "dve"))

            nt = len(tiles)
            tile_loads(0)
            if nt > 1:
                tile_loads(1)
            partA(0)
            for ti in range(nt):
                if ti + 2 < nt:
                    tile_loads(ti + 2)
                partB(ti)
                if ti + 1 < nt:
                    partA(ti + 1)
                partC(ti)
            P.barrier()

        tiles0 = [(128 * k, 128 * k) for k in range(T0 // 128)]
        tiles1 = [(b * BH + HALO + 128 * s2, b * BLK + 128 * s2) for b in range(NBLK) for s2 in range(2)]
        tiles2 = [(128 * k, 128 * k) for k in range(T1 // 128)]

        def run():
            if "KONLY" in os.environ:
                stage1(2)
                return
            stage1(0)
            if stop_after == "s1_0":
                return
            stage3(0, xin, h_a, tiles0, False)
            if stop_after == "L0":
                return
            stage1(1)
            stage3(1, h_a, h_b, tiles1, False)
            if stop_after == "L1":
                return
            stage1(2)
            if stop_after == "s1_2":
                dbg_k = nc.dram_tensor("dbg_k", [128, T1], BF16, kind="ExternalOutput").ap()
                dbg_v = nc.dram_tensor("dbg_v", [128, 260], BF16, kind="ExternalOutput").ap()
                dbg_km = nc.dram_tensor("dbg_km", [128, NCORES * NH * NBLK], BF16, kind="ExternalOutput").ap()
                dbg_q = nc.dram_tensor("dbg_q", [128, T1], BF16, kind="ExternalOutput").ap()
                dbg_sg = nc.dram_tensor("dbg_sg", [128, T1], BF16, kind="ExternalOutput").ap()
                AR.off = PERSIST_END
                tk = AR.alloc([T1], BF16)
                tv = AR.alloc([260], BF16)
                dma("sp", tk, kT_all[(3 * NH + 5) * 128:(3 * NH + 5) * 128 + 128, :], (), ["tk"])
                dma("sp", dbg_k, tk, ["tk"], ())
                r0 = ((3 * NH + 5) * NBLK + 2) * 128
                dma("sp", tv, V_all[r0:r0 + 128, :], (), ["tv"])
                dma("sp", dbg_v, tv, ["tv"], ())
                dma("sp", dbg_km, kmT.rearrange("d r x -> d (r x)"), (), ())
                tq = AR.alloc([T1], BF16)
                dma("sp", tq, QT_scr[:, 5, :], (), ["tq"])
                dma("sp", dbg_q, tq, ["tq"], ())
                tsg = AR.alloc([T1], BF16)
                dma("sp", tsg, SG_scr[:, 5, :], (), ["tsg"])
                dma("sp", dbg_sg, tsg, ["tsg"], ())
                return
            stage2(2)
            stage3(2, h_b, h_a, tiles2, False)
            if stop_after == "L2":
                return
            stage1(3)
            stage2(3)
            stage3(3, h_a, out, tiles2, True)

        run()
        nsem = P.emit(st)
        if debug:
            print("ops", len(P.ops), "sems", nsem, flush=True)
    return nc


def make_in_maps(inputs):
    x = np.ascontiguousarray(np.asarray(inputs["x"], dtype=np.float32)[0])
    mem = np.ascontiguousarray(np.asarray(inputs["mem"], dtype=np.float32)[0])
    shared = {
        "mem": mem,
        "w_in": np.ascontiguousarray(inputs["w_in"], dtype=np.float32),
        "w_out": np.ascontiguousarray(inputs["w_out"], dtype=np.float32),
        "w_mem_kv": np.ascontiguousarray(inputs["w_mem_kv"], dtype=np.float32),
        "ln_g": np.ascontiguousarray(inputs["ln_g"], dtype=np.float32),
        "ln_b": np.ascontiguousarray(inputs["ln_b"], dtype=np.float32),
        "pool_w": np.ascontiguousarray(inputs["pool_w"], dtype=np.float32),
        "w_kv_shared": np.ascontiguousarray(inputs["w_kv_shared"], dtype=np.float32),
    }
    ps = np.asarray(inputs["pool_scale"], dtype=np.float32)
    shared["t_pscale"] = np.ascontiguousarray(ps.reshape(2, 12, 128).transpose(2, 0, 1).reshape(128, 24))
    p_ = np.arange(128)[:, None, None]
    c_ = np.arange(2)[None, :, None]
    q_ = np.arange(256)[None, None, :]
    shared["t_tri"] = np.ascontiguousarray(((2 * p_ + c_) <= q_).astype(np.float32).reshape(128, 512))
    maps = []
    for c in range(NCORES):
        xin = np.zeros((T0, D), np.float32)
        for b in range(NBLK):
            g = gblock(c, b)
            lo = g * BLK - HALO
            if lo < 0:
                xin[b * BH + HALO:(b + 1) * BH] = x[0:BLK]
            else:
                xin[b * BH:(b + 1) * BH] = x[lo:lo + BH]
        hmask = np.ones((128, BH), np.float32)
        invcnt = np.zeros((4, BH), np.float32)
        for gi, w in enumerate(POOL_W):
            invcnt[gi, :] = 1.0 / w
        if gblock(c, 0) == 0:
            hmask[:, :HALO] = 0.0
            for gi, w in enumerate(POOL_W):
                t = np.arange(BLK)
                invcnt[gi, HALO:] = 1.0 / np.minimum(t + 1, w)
        invcnt = np.broadcast_to(invcnt.reshape(1, 4 * BH), (128, 4 * BH))
        pm = np.zeros((16, NSLOT), np.float32)
        p01 = np.zeros((16, NSLOT), np.float32)
        for t in range(16):
            gq = gblock(c, t // 2)
            for r in range(NCORES):
                for b in range(NBLK):
                    prior = gblock(r, b) < gq
                    pm[t, r * NBLK + b] = 0.0 if prior else NEG
                    p01[t, r * NBLK + b] = 1.0 if prior else 0.0
        m = dict(shared)
        m["xin"] = xin
        m["t_hmask"] = hmask
        m["t_invcnt"] = np.ascontiguousarray(invcnt)
        m["t_pm"] = np.ascontiguousarray(np.broadcast_to(pm.reshape(1, -1), (128, 16 * NSLOT)))
        m["t_p01"] = np.ascontiguousarray(np.broadcast_to(p01.reshape(1, -1), (128, 16 * NSLOT)))
        maps.append(m)
    return maps


_NC_CACHE = {}


def kernel(**inputs):
    maps = make_in_maps(inputs)
    if "nc" not in _NC_CACHE:
        _NC_CACHE["nc"] = build()
    res = run_bass_kernel_spmd(_NC_CACHE["nc"], maps, core_ids=list(range(NCORES)))
    outp = np.empty((1, SEQ, D), np.float32)
    for c in range(NCORES):
        o = res.results[c]["out"]
        for b in range(NBLK):
            g = gblock(c, b)
            outp[0, g * BLK:(g + 1) * BLK] = o[b * BLK:(b + 1) * BLK]
    return outp
```

```python
import contextlib
import os
import numpy as np
import concourse.bass as bass
import concourse.mybir as mybir
from concourse.bass_utils import run_bass_kernel_spmd

F32 = mybir.dt.float32
BF16 = mybir.dt.bfloat16
U8 = mybir.dt.uint8
ALU = mybir.AluOpType
AF = mybir.ActivationFunctionType
AX = mybir.AxisListType

NCORES = 8
D = 2048
SEQ = 16384
DEPTH = 4
NBLK = 8
BLK = 256
HALO = 32
BH = BLK + HALO
T0 = NBLK * BH
T1 = NBLK * BLK
NH = 12
NSLOT = 64
ALPHA = float((2 * DEPTH) ** 0.25)
LN_EPS = 1e-5
SCALE = float(128 ** -0.5)
POOL_W = (2, 4, 8, 16)
NEG = -1.0e30
ENGS = ("pe", "act", "dve", "pool", "sp")


class _Op:
    __slots__ = ("eng", "fn", "dma", "deps", "signal", "sem", "val", "qidx", "cc")

    def __init__(self, eng, fn, dma, cc=False):
        self.cc = cc
        self.eng = eng
        self.fn = fn
        self.dma = dma
        self.deps = {}
        self.signal = False
        self.sem = None
        self.val = 0
        self.qidx = -1


class Prog:
    ND = 8
    ROT = 30000

    def __init__(self, nc):
        self.nc = nc
        self.ops = []
        self.last_write = {}
        self.readers = {}

    def op(self, eng, fn, reads=(), writes=(), dma=False, cc=False):
        o = _Op(eng, fn, dma or cc, cc)
        for k in reads:
            w = self.last_write.get(k)
            if w is not None:
                o.deps[w] = True
        for k in writes:
            w = self.last_write.get(k)
            if w is not None:
                o.deps.setdefault(w, False)
            for r in self.readers.get(k, ()):
                if r is not o:
                    o.deps.setdefault(r, False)
        for k in reads:
            self.readers.setdefault(k, set()).add(o)
        for k in writes:
            self.last_write[k] = o
            self.readers[k] = set()
        self.ops.append(o)
        return o

    def barrier(self):
        lastc = {}
        dmas = {}
        for o in self.ops:
            if o.fn is None:
                continue
            if o.dma:
                dmas.setdefault(o.eng, []).append(o)
            else:
                lastc[o.eng] = o
        for e in ENGS:
            b = _Op(e, None, False)
            for e2, o in lastc.items():
                if e2 != e:
                    b.deps[o] = True
            for e2, lst in dmas.items():
                for o in lst[-(self.ND + 2):]:
                    b.deps[o] = True
                for o in lst:
                    if o.cc:
                        b.deps[o] = True
            self.ops.append(b)
        self.last_write = {}
        self.readers = {}

    def emit(self, stack):
        nc = self.nc
        ops = self.ops
        per_eng = {e: [o for o in ops if o.eng == e] for e in ENGS}
        for o in ops:
            need = {}
            for d, raw in o.deps.items():
                if d.dma:
                    need[d] = True
                elif d.eng == o.eng:
                    if o.eng == "pe":
                        continue
                    if o.dma or raw:
                        need[d] = True
                else:
                    need[d] = True
            o.deps = need
            for d in need:
                d.signal = True
        nsem = [0]

        def newsem(name):
            nsem[0] += 1
            return stack.enter_context(nc.semaphore(name))

        for e in ENGS:
            cnt = 0
            sem = None
            gen = 0
            nd = 0
            ncc = 0
            ccsem = None
            pool = None
            for o in per_eng[e]:
                if o.fn is None:
                    continue
                if o.cc:
                    if ccsem is None:
                        ccsem = newsem(f"cc_{e}")
                    ncc += 1
                    o.sem = ccsem
                    o.val = ncc
                elif o.dma:
                    if pool is None:
                        pool = [newsem(f"d_{e}_{j}") for j in range(self.ND)]
                    o.qidx = nd
                    o.sem = pool[nd % self.ND]
                    o.val = 16 * (nd // self.ND + 1)
                    nd += 1
                elif o.signal:
                    if sem is None or cnt >= self.ROT:
                        sem = newsem(f"c_{e}_{gen}")
                        gen += 1
                        cnt = 0
                    cnt += 1
                    o.sem = sem
                    o.val = cnt
        final_dma = {e: [o for o in per_eng[e] if o.dma and not o.cc] for e in ENGS}
        block = stack.enter_context(nc.Block())
        ND = self.ND

        def run_engine(e, eng):
            known = {}
            dmas = final_dma[e]
            for o in per_eng[e]:
                waits = {}
                for d in o.deps:
                    if known.get(d.sem, 0) >= d.val:
                        continue
                    if waits.get(d.sem, 0) < d.val:
                        waits[d.sem] = d.val
                if o.dma and not o.cc and o.qidx >= ND:
                    p = dmas[o.qidx - ND]
                    if known.get(p.sem, 0) < p.val and waits.get(p.sem, 0) < p.val:
                        waits[p.sem] = p.val
                for s, v in waits.items():
                    eng.wait_ge(s, v)
                    known[s] = v
                if o.fn is None:
                    continue
                ins = o.fn(eng)
                if o.cc:
                    ins.then_inc(o.sem, 1)
                elif o.dma:
                    ins.then_inc(o.sem, 16)
                elif o.signal:
                    ins.then_inc(o.sem, 1)
            if e == "sp":
                for q in ENGS:
                    last = {}
                    for o in final_dma[q]:
                        last[o.sem] = max(last.get(o.sem, 0), o.val)
                    for s, v in last.items():
                        if known.get(s, 0) < v:
                            eng.wait_ge(s, v)

        @block.tensor
        def _(eng):
            run_engine("pe", eng)

        @block.scalar
        def _(eng):
            run_engine("act", eng)

        @block.vector
        def _(eng):
            run_engine("dve", eng)

        @block.gpsimd
        def _(eng):
            run_engine("pool", eng)

        @block.sync
        def _(eng):
            run_engine("sp", eng)
        return nsem[0]


_DTSIZE = {F32: 4, BF16: 2, U8: 1}


class Arena:
    def __init__(self, ap, nbytes):
        self.ap = ap
        self.nbytes = nbytes
        self.off = 0

    def alloc(self, free_shape, dt):
        n = int(np.prod(free_shape)) * _DTSIZE[dt]
        off = (self.off + 63) // 64 * 64
        assert off + n <= self.nbytes, f"arena overflow {off + n} > {self.nbytes}"
        self.off = off + n
        v = self.ap[:, off:off + n].bitcast(dt)
        if len(free_shape) > 1:
            names = [chr(ord("a") + i) for i in range(len(free_shape))]
            pat = "p (" + " ".join(names) + ") -> p " + " ".join(names)
            v = v.rearrange(pat, **{nm: int(s) for nm, s in zip(names[1:], free_shape[1:])})
        return v


def gblock(c, b):
    i = b // 2
    return 16 * i + (c if b % 2 == 0 else 15 - c)


def build(stop_after=None, debug=False):
    nc = bass.Bass("TRN2", target_bir_lowering=False)
    okind = "ExternalOutput" if debug else "Internal"

    def din(name, shape, dt=F32):
        return nc.dram_tensor(name, list(shape), dt, kind="ExternalInput").ap()

    def dscr(name, shape, dt, kind="Internal"):
        return nc.dram_tensor(name, list(shape), dt, kind=kind).ap()

    xin = din("xin", [T0, D])
    mem = din("mem", [256, D])
    w_in = din("w_in", [DEPTH, D, 4096])
    w_out = din("w_out", [DEPTH, D, D])
    w_mkv = din("w_mem_kv", [DEPTH, D, 1024])
    ln_g = din("ln_g", [DEPTH, D])
    ln_b = din("ln_b", [DEPTH, D])
    pool_w = din("pool_w", [2, 4, 384, 384])
    w_kv = din("w_kv_shared", [D, 3072])
    t_pscale = din("t_pscale", [128, 24])
    t_hmask = din("t_hmask", [128, BH])
    t_invcnt = din("t_invcnt", [128, 4 * BH])
    t_pm = din("t_pm", [128, 16 * NSLOT])
    t_p01 = din("t_p01", [128, 16 * NSLOT])
    t_tri = din("t_tri", [128, 512])
    out = nc.dram_tensor("out", [T1, D], F32, kind="ExternalOutput").ap()

    hT_scr = dscr("hT_scr", [128, 16, T0], BF16)
    mixT_scr = dscr("mixT_scr", [128, 16, T0], BF16)
    h_a = dscr("h_a", [T0, D], F32, okind)
    h_b = dscr("h_b", [T1, D], F32, okind)
    QT_scr = dscr("QT_scr", [128, NH, T1], BF16)
    SG_scr = dscr("SG_scr", [128, NH, T1], BF16)
    kT_loc = dscr("kT_loc", [NH * 128, T1], BF16)
    V_loc = dscr("V_loc", [NH * NBLK * 128, 260], BF16)
    kmT_loc = dscr("kmT_loc", [128, NH * NBLK], BF16)
    kT_all = dscr("kT_all", [NCORES * NH * 128, T1], BF16)
    V_all = dscr("V_all", [NCORES * NH * NBLK * 128, 260], BF16)
    kmT_all = dscr("kmT_all", [NCORES * 128, NH * NBLK], BF16)

    st = contextlib.ExitStack()
    with st:
        ARENA_BYTES = 204 * 1024
        arena_t = st.enter_context(nc.sbuf_tensor("arena", [128, ARENA_BYTES], U8))
        AR = Arena(arena_t, ARENA_BYTES)
        pbig = [st.enter_context(nc.psum_tensor(f"pbig{i}", [128, 1024], F32)) for i in range(4)]
        psb = [pbig[i // 2][:, (i % 2) * 512:(i % 2 + 1) * 512] for i in range(8)]
        P = Prog(nc)
        ps_ctr = [0]

        def getps():
            i = ps_ctr[0] % 8
            ps_ctr[0] += 1
            return i

        def dma(eng, out_, in_, reads=(), writes=()):
            P.op(eng, lambda e: e.dma_start(out=out_, in_=in_), reads, writes, dma=True)

        def mm(out_, lhsT, rhs, start, stop, reads, writes):
            P.op("pe", lambda e: e.matmul(out_, lhsT=lhsT, rhs=rhs, start=start, stop=stop), reads, writes)

        def act(out_, in_, func, reads, writes, scale=None):
            if scale is None:
                P.op("act", lambda e: e.activation(out=out_, in_=in_, func=func), reads, writes)
            else:
                P.op("act", lambda e: e.activation(out=out_, in_=in_, func=func, scale=scale), reads, writes)

        def cp(eng, out_, in_, reads, writes):
            if eng == "act":
                act(out_, in_, AF.Copy, reads, writes)
            else:
                P.op(eng, lambda e: e.tensor_copy(out=out_, in_=in_), reads, writes)

        def tt(eng, out_, in0, in1, op, reads, writes):
            P.op(eng, lambda e: e.tensor_tensor(out=out_, in0=in0, in1=in1, op=op), reads, writes)

        def ts(eng, out_, in0, s1, s2, op0, op1, reads, writes):
            if op1 is None:
                P.op(eng, lambda e: e.tensor_scalar(out=out_, in0=in0, scalar1=s1, scalar2=None, op0=op0), reads, writes)
            else:
                P.op(eng, lambda e: e.tensor_scalar(out=out_, in0=in0, scalar1=s1, scalar2=s2, op0=op0, op1=op1),
                     reads, writes)

        def stt(eng, out_, in0, scalar, in1, op0, op1, reads, writes):
            P.op(eng, lambda e: e.scalar_tensor_tensor(out=out_, in0=in0, scalar=scalar, in1=in1, op0=op0, op1=op1),
                 reads, writes)

        def memset(eng, ap, val, writes):
            P.op(eng, lambda e: e.memset(ap, val), (), writes)

        ident = AR.alloc([128], BF16)
        identf = AR.alloc([128], F32)
        ones = AR.alloc([128], BF16)
        memT = AR.alloc([16, 256], BF16)
        hmask = AR.alloc([BH], F32)
        invcnt = AR.alloc([4, BH], F32)
        pm16 = AR.alloc([16, NSLOT], F32)
        p01 = AR.alloc([16, NSLOT], F32)
        tri = AR.alloc([2, 256], BF16)
        pscale = AR.alloc([24], F32)
        kmT = AR.alloc([NCORES, NH * NBLK], BF16)
        PERSIST_END = AR.off

        memset("pool", identf, 1.0, ["identf"])
        P.op("pool", lambda e: e.affine_select(out=identf, in_=identf, pattern=[[-1, 128]], compare_op=ALU.is_equal,
                                              fill=0.0, base=0, channel_multiplier=1), ["identf"], ["identf"])
        cp("dve", ident, identf, ["identf"], ["ident"])
        memset("pool", ones, 1.0, ["ones"])
        dma("sp", hmask, t_hmask, (), ["hmask"])
        dma("sp", invcnt, t_invcnt.rearrange("p (g t) -> p g t", t=BH), (), ["invcnt"])
        dma("sp", pm16, t_pm.rearrange("p (a s) -> p a s", s=NSLOT), (), ["pm16"])
        dma("sp", p01, t_p01.rearrange("p (a s) -> p a s", s=NSLOT), (), ["p01"])
        dma("pool", tri, t_tri.rearrange("p (c q) -> p c q", q=256), (), ["tri"])
        dma("sp", pscale, t_pscale, (), ["pscale"])

        def transpose_store(src_bf, src_keys, hTt, hTt_key, tok, evac_engs=("act", "dve")):
            for half in range(2):
                pi = getps()
                pv = psb[pi][:].bitcast(BF16).rearrange("p (a b) -> p a b", b=128)
                for j in range(8):
                    dc = half * 8 + j
                    P.op("pe", lambda e, o_=pv[:, j, :], i_=src_bf[:, dc * 128:(dc + 1) * 128]: e.transpose(o_, i_, ident),
                         list(src_keys) + ["ident"], [f"ps{pi}"])
                cp(evac_engs[half], hTt[:, half * 8:(half + 1) * 8, :], pv, [f"ps{pi}"], [hTt_key + str(half)])
            dma("sp", hT_scr[:, :, tok:tok + 128], hTt, [hTt_key + "0", hTt_key + "1"], [("hT_scr", tok)])

        AR.off = PERSIST_END
        x32 = [AR.alloc([D], F32) for _ in range(2)]
        xbf = [AR.alloc([D], BF16) for _ in range(2)]
        hTt0 = [AR.alloc([16, 128], BF16) for _ in range(2)]
        for mt_ in range(2):
            s = mt_
            dma("sp", x32[s], mem[mt_ * 128:(mt_ + 1) * 128, :], (), [f"x32_{s}"])
            cp("dve", xbf[s][:, 0:1024], x32[s][:, 0:1024], [f"x32_{s}"], [f"xbfa{s}"])
            cp("pool", xbf[s][:, 1024:2048], x32[s][:, 1024:2048], [f"x32_{s}"], [f"xbfb{s}"])
            for half in range(2):
                pi = getps()
                pv = psb[pi][:].bitcast(BF16).rearrange("p (a b) -> p a b", b=128)
                for j in range(8):
                    dc = half * 8 + j
                    P.op("pe", lambda e, o_=pv[:, j, :], i_=xbf[s][:, dc * 128:(dc + 1) * 128]: e.transpose(o_, i_, ident),
                         [f"xbfa{s}", f"xbfb{s}", "ident"], [f"ps{pi}"])
                cp("act", memT[:, half * 8:(half + 1) * 8, mt_ * 128:(mt_ + 1) * 128], pv, [f"ps{pi}"], ["memT"])
        dma("sp", x32[0], xin[0:128, :], (), ["x32_0"])
        for t in range(T0 // 128):
            s = t % 2
            if t + 1 < T0 // 128:
                dma("sp", x32[1 - s], xin[(t + 1) * 128:(t + 2) * 128, :], (), [f"x32_{1 - s}"])
            cp("dve", xbf[s][:, 0:1024], x32[s][:, 0:1024], [f"x32_{s}"], [f"xbfa{s}"])
            cp("pool", xbf[s][:, 1024:2048], x32[s][:, 1024:2048], [f"x32_{s}"], [f"xbfb{s}"])
            transpose_store(xbf[s], [f"xbfa{s}", f"xbfb{s}"], hTt0[s], f"hTt{s}_", t * 128)
        P.barrier()

        def stage1(L):
            pool_layer = L < 2
            Tin = T0 if pool_layer else T1
            N = BH if pool_layer else 512
            nblk = Tin // N
            AR.off = PERSIST_END
            hT = AR.alloc([16, Tin], BF16)
            WU = 768
            wu = [AR.alloc([16, WU], BF16) for _ in range(2)]
            pw = [AR.alloc([3, 384], BF16) for _ in range(2)]
            mkT = AR.alloc([4, 256], BF16)
            mv = AR.alloc([2, 512], BF16)
            for q in range(4):
                dma("sp", hT[:, 4 * q:4 * q + 4, :], hT_scr[:, 4 * q:4 * q + 4, 0:Tin], (), [f"hT{q}"])
            wctr = [0]
            specs = [(w_mkv[L], [(0, 512)], None), (w_mkv[L], [(512, 512)], None)]
            if L == 2:
                specs += [(w_kv, [(512 * u, 512)], None) for u in range(3)]
                specs += [(w_kv, [(1536 + 512 * u, 512)], None) for u in range(3)]
            if pool_layer:
                specs += [(w_in[L], [(384 * g, 384), (1536 + 384 * g, 384)], g) for g in range(4)]
            else:
                specs += [(w_in[L], [(512 * u, 512)], None) for u in range(3)]
                specs += [(w_in[L], [(1536 + 512 * u, 512)], None) for u in range(3)]
            specs += [(w_in[L], [(3072 + 256 * m, 256), (3584 + 256 * m, 256)], None) for m in range(2)]

            def issue_w(k):
                w2d, col_slices, g = specs[k]
                bi = k % 2
                off = 0
                for (c0, n) in col_slices:
                    dma("pool", wu[bi][:, :, off:off + n], w2d[:, c0:c0 + n].rearrange("(dc p) n -> p dc n", p=128),
                        (), [f"wu{bi}"])
                    off += n
                if g is not None:
                    dma("pool", pw[bi], pool_w[L, g].rearrange("(c p) d -> p c d", p=128), (), [f"pw{bi}"])

            def load_w(w2d, col_slices):
                k = wctr[0]
                assert specs[k][1] == col_slices, (k, specs[k][1], col_slices)
                if k == 0:
                    issue_w(0)
                if k + 1 < len(specs):
                    issue_w(k + 1)
                wctr[0] += 1
                return k % 2

            def proj_chunk(bi, widx, tok0, n):
                pi = getps()
                for dc in range(16):
                    mm(psb[pi][:, 0:n], wu[bi][:, dc, widx * 128:(widx + 1) * 128], hT[:, dc, tok0:tok0 + n],
                       dc == 0, dc == 15, [f"wu{bi}", f"hT{dc // 4}"], [f"ps{pi}"])
                return pi

            bi = load_w(w_mkv[L], [(0, 512)])
            for h in range(4):
                pi = getps()
                for dc in range(16):
                    mm(psb[pi][:, 0:256], wu[bi][:, dc, h * 128:(h + 1) * 128], memT[:, dc, :], dc == 0, dc == 15,
                       [f"wu{bi}", "memT"], [f"ps{pi}"])
                cp("act", mkT[:, h, :], psb[pi][:, 0:256], [f"ps{pi}"], ["mkT"])
            bi = load_w(w_mkv[L], [(512, 512)])
            for mc in range(2):
                pi = getps()
                for dc in range(16):
                    mm(psb[pi][:, 0:512], memT[:, dc, mc * 128:(mc + 1) * 128], wu[bi][:, dc, 0:512], dc == 0, dc == 15,
                       [f"wu{bi}", "memT"], [f"ps{pi}"])
                cp("act", mv[:, mc, :], psb[pi][:, 0:512], [f"ps{pi}"], ["mv"])

            if L == 2:
                kst = [AR.alloc([2, 2, 128], BF16) for _ in range(3)]
                km32 = AR.alloc([NH, NBLK], F32)
                kmb = AR.alloc([NH * NBLK], BF16)
                memset("pool", km32, 0.0, ["km32z"])
                vst = [AR.alloc([4, 2, 130], BF16) for _ in range(2)]
                for s in range(2):
                    memset("pool", vst[s][:, :, :, 128:129], 1.0, [f"vst{s}"])
                    memset("pool", vst[s][:, :, :, 129:130], 0.0, [f"vst{s}"])
                kctr = 0
                for u in range(0 if "SKIPK" not in os.environ else 3, 3):
                    bi = load_w(w_kv, [(512 * u, 512)])
                    for blk in range(4):
                        tok0 = blk * 512
                        for hh in range(4):
                            h = 4 * u + hh
                            pi = proj_chunk(bi, hh, tok0, 512)
                            ks = kctr % 3
                            kctr += 1
                            for b2 in range(2):
                                src_ = psb[pi][:, b2 * 256:(b2 + 1) * 256].rearrange("d (p c) -> d c p", c=2)
                                P.op("act", lambda e, o_=kst[ks][:, b2, :, :], i_=src_,
                                     a_=km32[:, h, 2 * blk + b2:2 * blk + b2 + 1]:
                                     e.activation(out=o_, in_=i_, func=AF.Copy, accum_out=a_),
                                     [f"ps{pi}", "km32z"], [f"kst{ks}", ("km32", h, 2 * blk + b2)])
                            dma("sp", kT_loc[h * 128:(h + 1) * 128, tok0:tok0 + 512],
                                kst[ks].rearrange("d b c p -> d (b c p)"), [f"kst{ks}"], ["kT_loc"])
                ts("dve", kmb, km32.rearrange("p h b -> p (h b)"), 1.0 / 256.0, None, ALU.mult, None,
                   [("km32", h_, b_) for h_ in range(NH) for b_ in range(NBLK)], ["kmb"])
                dma("sp", kmT_loc, kmb, ["kmb"], ["kmT_loc"])
                vctr = 0
                for u in range(0 if "SKIPV" not in os.environ else 3, 3):
                    bi = load_w(w_kv, [(1536 + 512 * u, 512)])
                    for b in range(NBLK):
                        vs = vctr % 2
                        vctr += 1
                        for c in range(2):
                            pi = getps()
                            for dc in range(16):
                                mm(psb[pi][:, 0:512], hT[:, dc, b * 256 + c:(b + 1) * 256:2], wu[bi][:, dc, 0:512],
                                   dc == 0, dc == 15, [f"wu{bi}", f"hT{dc // 4}"], [f"ps{pi}"])
                            cp("act" if c == 0 else "dve", vst[vs][:, :, c, 0:128],
                               psb[pi][:, 0:512].rearrange("p (h d) -> p h d", d=128), [f"ps{pi}"], [f"vst{vs}"])
                        for hh in range(4):
                            h = 4 * u + hh
                            r0 = (h * NBLK + b) * 128
                            dma("sp", V_loc[r0:r0 + 128, :], vst[vs][:, hh, :, :].rearrange("p c x -> p (c x)"),
                                [f"vst{vs}"], ["V_loc"])
                grp = [list(range(NCORES))]
                if "NOAG" in os.environ:
                    grp = None
                if grp is not None:
                  P.op("pool", lambda e: e.collective_compute("AllGather", ALU.bypass, replica_groups=grp,
                                                            ins=[kT_loc.opt()], outs=[kT_all.opt()]),
                     ["kT_loc"], ["kT_all"], cc=True)
                if grp is not None:
                  P.op("pool", lambda e: e.collective_compute("AllGather", ALU.bypass, replica_groups=grp,
                                                            ins=[V_loc.opt()], outs=[V_all.opt()]),
                     ["V_loc"], ["V_all"], cc=True)
                if grp is not None:
                  P.op("pool", lambda e: e.collective_compute("AllGather", ALU.bypass, replica_groups=grp,
                                                            ins=[kmT_loc.opt()], outs=[kmT_all.opt()]),
                     ["kmT_loc"], ["kmT_all"], cc=True)

            if pool_layer:
                uU = [AR.alloc([3, BH], F32) for _ in range(2)]
                uA = [AR.alloc([3, BH], F32) for _ in range(2)]
                uB = [AR.alloc([3, BH], F32) for _ in range(2)]
                pl = [AR.alloc([3, BH], BF16) for _ in range(2)]
                sg = [AR.alloc([3, BH], BF16) for _ in range(2)]
                mx = [AR.alloc([3, BH], BF16) for _ in range(2)]
                tmpc = AR.alloc([BH], F32)
                for s in range(2):
                    memset("pool", uA[s], 0.0, [f"uA{s}"])
                    memset("pool", uB[s], 0.0, [f"uB{s}"])
            else:
                qst = [AR.alloc([512], BF16) for _ in range(4)]
            qm = [AR.alloc([N], BF16) for _ in range(2)]
            pTm = [AR.alloc([2, N], BF16) for _ in range(2)]
            sgm = [AR.alloc([N], F32) for _ in range(2)]
            rdm = [AR.alloc([N], F32) for _ in range(2)]
            t1m = [AR.alloc([N], F32) for _ in range(2)]
            mxm = [AR.alloc([N], BF16) for _ in range(2)]

            if pool_layer:
                cnt = 0
                for g in range(4):
                    w = POOL_W[g]
                    bi = load_w(w_in[L], [(384 * g, 384), (1536 + 384 * g, 384)])
                    for b in range(NBLK):
                        s = cnt % 2
                        cnt += 1
                        tok0 = b * BH
                        U, A, B = uU[s], uA[s], uB[s]
                        for c in range(3):
                            pi = proj_chunk(bi, c, tok0, BH)
                            if b == 0:
                                tt("dve", U[:, c, :], psb[pi][:, 0:BH], hmask, ALU.mult, [f"ps{pi}", "hmask"], [f"uU{s}"])
                            else:
                                cp("act", U[:, c, :], psb[pi][:, 0:BH], [f"ps{pi}"], [f"uU{s}"])
                        tt("dve", A[:, :, 1:], U[:, :, 1:], U[:, :, 0:BH - 1], ALU.add, [f"uU{s}"], [f"uA{s}"])
                        S_, Sk = A, f"uA{s}"
                        if w >= 4:
                            tt("pool", B[:, :, 3:], A[:, :, 3:], A[:, :, 1:BH - 2], ALU.add, [f"uA{s}"], [f"uB{s}"])
                            S_, Sk = B, f"uB{s}"
                        if w >= 8:
                            tt("dve", A[:, :, 7:], B[:, :, 7:], B[:, :, 3:BH - 4], ALU.add, [f"uB{s}"], [f"uA{s}"])
                            S_, Sk = A, f"uA{s}"
                        if w >= 16:
                            tt("pool", B[:, :, 15:], A[:, :, 15:], A[:, :, 7:BH - 8], ALU.add, [f"uA{s}"], [f"uB{s}"])
                            S_, Sk = B, f"uB{s}"
                        if b == 0:
                            for c in range(3):
                                tt("dve", tmpc, S_[:, c, :], invcnt[:, g, :], ALU.mult, [Sk, "invcnt"], ["tmpc"])
                                tt("dve", pl[s][:, c, :], tmpc, U[:, c, :], ALU.subtract, ["tmpc", f"uU{s}"], [f"pl{s}"])
                        else:
                            stt("dve", pl[s], S_, 1.0 / w, U, ALU.mult, ALU.subtract, [Sk, f"uU{s}"], [f"pl{s}"])
                        for j in range(3):
                            pi = proj_chunk(bi, 3 + j, tok0, BH)
                            act(sg[s][:, j, :], psb[pi][:, 0:BH], AF.Silu, [f"ps{pi}"], [f"sg{s}"])
                        pjs = []
                        for j in range(3):
                            pj = getps()
                            pjs.append(pj)
                            for c in range(3):
                                mm(psb[pj][:, 0:BH], pw[bi][:, c, j * 128:(j + 1) * 128], pl[s][:, c, :], c == 0, c == 2,
                                   [f"pw{bi}", f"pl{s}"], [f"ps{pj}"])
                        for j in range(3):
                            k = L * 12 + 3 * g + j
                            stt("dve", mx[s][:, j, :], psb[pjs[j]][:, 0:BH], pscale[:, k:k + 1], sg[s][:, j, :], ALU.mult,
                                ALU.mult, [f"ps{pjs[j]}", "pscale", f"sg{s}"], [f"mx{s}"])
                        dma("sp", mixT_scr[:, 3 * g:3 * g + 3, tok0:tok0 + BH], mx[s], [f"mx{s}"], [("mixT", g, b)])
            else:
                qc = 0
                for u in range(0 if "SKIPQ" not in os.environ else 3, 3):
                    bi = load_w(w_in[L], [(512 * u, 512)])
                    for blk in range(4):
                        tok0 = blk * 512
                        for hh in range(4):
                            h = 4 * u + hh
                            pi = proj_chunk(bi, hh, tok0, 512)
                            s = qc % 4
                            qc += 1
                            cp("act" if qc % 2 else "dve", qst[s], psb[pi][:, 0:512], [f"ps{pi}"], [f"qst{s}"])
                            dma("sp", QT_scr[:, h, tok0:tok0 + 512], qst[s], [f"qst{s}"], [("QT", h, blk)])
                for u in range(3):
                    bi = load_w(w_in[L], [(1536 + 512 * u, 512)])
                    for blk in range(4):
                        tok0 = blk * 512
                        for hh in range(4):
                            h = 4 * u + hh
                            pi = proj_chunk(bi, hh, tok0, 512)
                            s = qc % 4
                            qc += 1
                            act(qst[s], psb[pi][:, 0:512], AF.Silu, [f"ps{pi}"], [f"qst{s}"])
                            dma("sp", SG_scr[:, h, tok0:tok0 + 512], qst[s], [f"qst{s}"], [("SG", h, blk)])

            cnt = 0
            for m in range(0 if "SKIPM" not in os.environ else 2, 2):
                bi = load_w(w_in[L], [(3072 + 256 * m, 256), (3584 + 256 * m, 256)])
                for blk in range(nblk):
                    tok0 = blk * N
                    for hh in range(2):
                        h = 2 * m + hh
                        s = cnt % 2
                        cnt += 1
                        pi = proj_chunk(bi, hh, tok0, N)
                        cp("dve", qm[s][:, 0:N], psb[pi][:, 0:N], [f"ps{pi}"], [f"qm{s}"])
                        for mc in range(2):
                            p_s = getps()
                            mm(psb[p_s][:, 0:N], mkT[:, h, mc * 128:(mc + 1) * 128], qm[s][:, 0:N], True, True,
                               ["mkT", f"qm{s}"], [f"ps{p_s}"])
                            act(pTm[s][:, mc, 0:N], psb[p_s][:, 0:N], AF.Exp, [f"ps{p_s}"], [f"pTm{s}"], scale=SCALE)
                        p_o = getps()
                        for mc in range(2):
                            mm(psb[p_o][:, 0:N], mv[:, mc, h * 128:(h + 1) * 128], pTm[s][:, mc, 0:N], mc == 0, mc == 1,
                               ["mv", f"pTm{s}"], [f"ps{p_o}"])
                        p_d = getps()
                        for mc in range(2):
                            mm(psb[p_d][:, 0:N], ones, pTm[s][:, mc, 0:N], mc == 0, mc == 1, ["ones", f"pTm{s}"],
                               [f"ps{p_d}"])
                        p_g = proj_chunk(bi, 2 + hh, tok0, N)
                        act(sgm[s][:, 0:N], psb[p_g][:, 0:N], AF.Silu, [f"ps{p_g}"], [f"sgm{s}"])
                        P.op("dve", lambda e, o_=rdm[s][:, 0:N], i_=psb[p_d][:, 0:N]: e.reciprocal(out=o_, in_=i_),
                             [f"ps{p_d}"], [f"rdm{s}"])
                        tt("dve", t1m[s][:, 0:N], psb[p_o][:, 0:N], rdm[s][:, 0:N], ALU.mult, [f"ps{p_o}", f"rdm{s}"],
                           [f"t1m{s}"])
                        tt("pool", mxm[s][:, 0:N], t1m[s][:, 0:N], sgm[s][:, 0:N], ALU.mult, [f"t1m{s}", f"sgm{s}"],
                           [f"mxm{s}"])
                        dma("sp", mixT_scr[:, 12 + h, tok0:tok0 + N], mxm[s][:, 0:N], [f"mxm{s}"], [("mixT", 12 + h, blk)])
            P.barrier()

        def stage2(L):
            AR.off = PERSIST_END
            kT = AR.alloc([NCORES, NBLK, 256], BF16)
            vv = AR.alloc([NCORES, NBLK, 260], BF16)
            kTo = [AR.alloc([NBLK, 256], BF16) for _ in range(2)]
            vo = [AR.alloc([NBLK, 260], BF16) for _ in range(2)]
            qT = [AR.alloc([T1], BF16) for _ in range(2)]
            sgT = [AR.alloc([T1], BF16) for _ in range(2)]
            mxh = [AR.alloc([T1], BF16) for _ in range(2)]
            sel = [AR.alloc([16, NSLOT], F32) for _ in range(2)]
            mg = AR.alloc([8, NSLOT], F32)
            top8 = AR.alloc([16, 8], F32)
            acc = [AR.alloc([4, 129], F32) for _ in range(2)]
            NPT = 5
            pT = [AR.alloc([2, 256], BF16) for _ in range(NPT)]
            rc = AR.alloc([16], F32)
            obf = [AR.alloc([128], BF16) for _ in range(2)]
            cnt = {"S": 0, "O": 0, "pT": 0, "obf": 0}
            DEPTH_ = 2
            if L == 2:
                dma("sp", kmT, kmT_all.rearrange("(r d) x -> d r x", d=128), (), ["kmT"])

            def nxt(name, n):
                v = cnt[name] % n
                cnt[name] += 1
                return v

            def Sbank(k):
                return psb[k]

            def Obank(k):
                return psb[4 + k]

            def small_loads(h_):
                hs_ = h_ % 2
                dma("sp", kTo[hs_], kT_loc[h_ * 128:(h_ + 1) * 128, :].rearrange("d (b x) -> d b x", x=256), ["kT_loc"],
                    [f"kTo{hs_}"])
                dma("sp", vo[hs_], V_loc.rearrange("(hh b p) x -> hh p b x", hh=NH, b=NBLK)[h_], ["V_loc"], [f"vo{hs_}"])
                dma("sp", qT[hs_], QT_scr[:, h_, :], [("QT", h_, k) for k in range(4)], [f"qT{hs_}"])
                dma("sp", sgT[hs_], SG_scr[:, h_, :], [("SG", h_, k) for k in range(4)], [f"sgT{hs_}"])

            def kv_group_load(h_, i):
                src = kT_all.rearrange("(r hd) (b x) -> hd r b x", r=NCORES, x=256)[h_ * 128:(h_ + 1) * 128, :, 2 * i:2 * i + 2, :]
                dma("sp", kT[:, :, 2 * i:2 * i + 2, :], src, ["kT_all"], [f"kT_g{i}"])
                srcv = V_all.rearrange("(r hh b p) x -> hh p r b x", r=NCORES, hh=NH, b=NBLK)[h_, :, :, 2 * i:2 * i + 2, :]
                for b2 in range(2):
                    dma("sp", vv[:, :, 2 * i + b2, :], srcv[:, :, b2, :], ["V_all"], [f"v_g{i}"])

            for h in range(NH):
                hs = h % 2
                if h == 0:
                    small_loads(0)
                    for i in range(3, -1, -1):
                        kv_group_load(0, i)
                kmh = kmT.rearrange("d r (hh b) -> d hh r b", b=NBLK)[:, h, :, :]
                for half in range(2):
                    sk = nxt("S", 4)
                    pv = Sbank(sk)[:, 0:512].rearrange("p (a s) -> p a s", s=NSLOT)
                    for t8 in range(8):
                        t = half * 8 + t8
                        mm(pv[:, t8, :].rearrange("p (r b) -> p r b", b=NBLK), qT[hs][:, t * 128:(t + 1) * 128], kmh,
                           True, True, [f"qT{hs}", "kmT"], [f"S{sk}"])
                    tt("dve", mg, pv, pm16[:, half * 8:(half + 1) * 8, :], ALU.add, [f"S{sk}", "pm16"], ["mg"])
                    for t8 in range(8):
                        t = half * 8 + t8
                        P.op("dve", lambda e, o_=top8[:, t, :], i_=mg[:, t8, :]: e.max(out=o_, in_=i_), ["mg"], [("top8", t)])
                        stt("dve", sel[hs][:, t, :], mg[:, t8, :], top8[:, t, 2:3], p01[:, t, :], ALU.is_ge, ALU.mult,
                            ["mg", ("top8", t), "p01"], [f"sel{hs}"])
                if h + 1 < NH:
                    small_loads(h + 1)
                items = []
                for i in range(3, -1, -1):
                    items.append({"k": "own", "i": i, "X": 0})
                    items.append({"k": "own", "i": i, "X": 1})
                    for r in range(NCORES):
                        for b in range(2 * i + 2):
                            if b != 2 * i + 1:
                                items.append({"k": "g", "i": i, "r": r, "b": b, "X": 0})
                            items.append({"k": "g", "i": i, "r": r, "b": b, "X": 1})
                    items[-1]["last"] = True

                def front(it):
                    i, X = it["i"], it["X"]
                    bq = 2 * i + X
                    sk = nxt("S", 4)
                    S_ = Sbank(sk)
                    qblk = qT[hs][:, bq * 256:(bq + 1) * 256]
                    ps_ = nxt("pT", NPT)
                    it["p"] = ps_
                    if it["k"] == "own":
                        for c in range(2):
                            mm(S_[:, c * 256:(c + 1) * 256], kTo[hs][:, bq, c * 128:(c + 1) * 128], qblk, True, True,
                               [f"kTo{hs}", f"qT{hs}"], [f"S{sk}"])
                        act(pT[ps_], S_[:, 0:512].rearrange("p (c q) -> p c q", q=256), AF.Exp, [f"S{sk}"],
                            [f"pT{ps_}"], scale=SCALE)
                        tt("pool", pT[ps_], pT[ps_], tri, ALU.mult, [f"pT{ps_}", "tri"], [f"pT{ps_}"])
                    else:
                        r, b = it["r"], it["b"]
                        g_ = b // 2
                        for c in range(2):
                            mm(S_[:, c * 256:(c + 1) * 256], kT[:, r, b, c * 128:(c + 1) * 128], qblk, True, True,
                               [f"kT_g{g_}", f"qT{hs}"], [f"S{sk}"])
                        act(pT[ps_], S_[:, 0:512].rearrange("p (c q) -> p c q", q=256), AF.Exp, [f"S{sk}"],
                            [f"pT{ps_}"], scale=SCALE)

                def back(it):
                    i, X = it["i"], it["X"]
                    a_s = i % 2
                    A_ = acc[a_s]
                    ak = f"acc{a_s}"
                    ok = nxt("O", 4)
                    O = Obank(ok)[:, 0:258].rearrange("p (s x) -> p s x", x=129)
                    pt = pT[it["p"]]
                    if it["k"] == "own":
                        bq = 2 * i + X
                        for s2 in range(2):
                            for c in range(2):
                                mm(O[:, s2, :], pt[:, c, s2 * 128:(s2 + 1) * 128], vo[hs][:, bq, c * 130:c * 130 + 129],
                                   c == 0, c == 1, [f"pT{it['p']}", f"vo{hs}"], [f"O{ok}"])
                        cp("act", A_[:, 2 * X:2 * X + 2, :], O, [f"O{ok}"], [f"{ak}_{2 * X}", f"{ak}_{2 * X + 1}"])
                    else:
                        r, b = it["r"], it["b"]
                        g_ = b // 2
                        slot = r * NBLK + b
                        for s2 in range(2):
                            for c in range(2):
                                mm(O[:, s2, :], pt[:, c, s2 * 128:(s2 + 1) * 128], vv[:, r, b, c * 130:c * 130 + 129],
                                   c == 0, c == 1, [f"pT{it['p']}", f"v_g{g_}"], [f"O{ok}"])
                        for s2 in range(2):
                            qt = 2 * X + s2
                            t = 4 * i + qt
                            stt("dve", A_[:, qt, :], O[:, s2, :], sel[hs][:, t, slot:slot + 1], A_[:, qt, :],
                                ALU.mult, ALU.add, [f"O{ok}", f"sel{hs}", f"{ak}_{qt}"], [f"{ak}_{qt}"])
                    if it.get("last") and h + 1 < NH:
                        kv_group_load(h + 1, i)
                    if it.get("last"):
                        for qt in range(4):
                            t = 4 * i + qt
                            P.op("dve", lambda e, o_=rc[:, t:t + 1], i_=A_[:, qt, 128:129]: e.reciprocal(out=o_, in_=i_),
                                 [f"{ak}_{qt}"], [("rc", t)])
                            os_ = nxt("obf", 2)
                            ts("dve", obf[os_], A_[:, qt, 0:128], rc[:, t:t + 1], None, ALU.mult, None,
                               [f"{ak}_{qt}", ("rc", t)], [f"obf{os_}"])
                            ok2 = nxt("O", 4)
                            ptv = Obank(ok2)[:, 0:512].bitcast(BF16)[:, 0:128]
                            P.op("pe", lambda e, o_=ptv, i_=obf[os_]: e.transpose(o_, i_, ident), [f"obf{os_}", "ident"],
                                 [f"O{ok2}"])
                            tt("dve", mxh[hs][:, t * 128:(t + 1) * 128], ptv, sgT[hs][:, t * 128:(t + 1) * 128], ALU.mult,
                               [f"O{ok2}", f"sgT{hs}"], [f"mxh{hs}"])

                n_it = len(items)
                for idx in range(n_it + DEPTH_):
                    if idx < n_it:
                        front(items[idx])
                    if idx >= DEPTH_:
                        back(items[idx - DEPTH_])
                dma("sp", mixT_scr[:, h, 0:T1], mxh[hs], [f"mxh{hs}"], [("mixT", h)])
            P.barrier()

        def stage3(L, hsrc, hdst, tiles, last):
            AR.off = PERSIST_END
            wo = AR.alloc([16, D], BF16)
            gb = AR.alloc([2, D], F32)
            mt = [AR.alloc([16, 128], BF16) for _ in range(2)]
            h32 = [AR.alloc([D], F32) for _ in range(2)]
            z = [AR.alloc([D], F32) for _ in range(2)]
            hn = [AR.alloc([D], F32) for _ in range(2)]
            hb = [AR.alloc([D], BF16) for _ in range(2)]
            hTt = [AR.alloc([16, 128], BF16) for _ in range(2)]
            bst = [AR.alloc([4, 6], F32) for _ in range(2)]
            mvr = [AR.alloc([4], F32) for _ in range(2)]
            for q in range(4):
                dma("pool", wo[:, 4 * q:4 * q + 4, :],
                    w_out[L, 512 * q:512 * (q + 1), :].rearrange("(fc p) n -> p fc n", p=128), (), [f"wo{q}"])
            dma("sp", gb[:, 0, :], ln_g[L:L + 1, :].partition_broadcast(128), (), ["gb0"])
            dma("sp", gb[:, 1, :], ln_b[L:L + 1, :].partition_broadcast(128), (), ["gb1"])
            def tile_loads(ti_):
                s_ = ti_ % 2
                src_ = tiles[ti_][0]
                dma("sp", mt[s_], mixT_scr[:, :, src_:src_ + 128], (), [f"mt{s_}"])
                dma("sp", h32[s_], hsrc[src_:src_ + 128, :], (), [f"h32{s_}"])

            def partA(ti):
                s = ti % 2
                for og in range(4):
                    pi = getps()
                    for fc in range(16):
                        mm(psb[pi][:, 0:512], mt[s][:, fc, :], wo[:, fc, og * 512:(og + 1) * 512], fc == 0, fc == 15,
                           [f"mt{s}", f"wo{fc // 4}"], [f"ps{pi}"])
                    stt("dve", z[s][:, og * 512:(og + 1) * 512], h32[s][:, og * 512:(og + 1) * 512], ALPHA,
                        psb[pi][:, 0:512], ALU.mult, ALU.add, [f"h32{s}", f"ps{pi}"], [f"z{s}_{og}"])
                    P.op("dve", lambda e, o_=bst[s][:, og, :], i_=z[s][:, og * 512:(og + 1) * 512]: e.bn_stats(out=o_, in_=i_),
                         [f"z{s}_{og}"], [f"bst{s}"])

            def partB(ti):
                s = ti % 2
                dst = tiles[ti][1]
                P.op("dve", lambda e, o_=mvr[s][:, 0:2], i_=bst[s].rearrange("p a b -> p (a b)"): e.bn_aggr(out=o_, in_=i_),
                     [f"bst{s}"], [f"mvr{s}"])
                ts("dve", mvr[s][:, 3:4], mvr[s][:, 1:2], LN_EPS, None, ALU.add, None, [f"mvr{s}"], [f"mve{s}"])
                act(mvr[s][:, 2:3], mvr[s][:, 3:4], AF.Sqrt, [f"mve{s}"], [f"msd{s}"])
                P.op("dve", lambda e, o_=mvr[s][:, 2:3], i_=mvr[s][:, 2:3]: e.reciprocal(out=o_, in_=i_), [f"msd{s}"], [f"mrs{s}"])
                zk = [f"z{s}_{og}" for og in range(4)]
                ts("dve", z[s], z[s], mvr[s][:, 0:1], mvr[s][:, 2:3], ALU.subtract, ALU.mult, zk + [f"mvr{s}", f"mrs{s}"], zk)
                tt("pool", hn[s], z[s], gb[:, 0, :], ALU.mult, zk + ["gb0"], [f"hn{s}"])
                tt("pool", hn[s], hn[s], gb[:, 1, :], ALU.add, [f"hn{s}", "gb1"], [f"hn{s}"])
                dma("sp", hdst[dst:dst + 128, :], hn[s], [f"hn{s}"], [("hdst", dst)])
                if not last:
                    cp("act", hb[s], hn[s], [f"hn{s}"], [f"hb{s}"])

            def partC(ti):
                s = ti % 2
                dst = tiles[ti][1]
                if not last:
                    transpose_store(hb[s], [f"hb{s}"], hTt[s], f"hTt{s}_", dst, evac_engs=("act", "dve"))

            nt = len(tiles)
            tile_loads(0)
            if nt > 1:
                tile_loads(1)
            partA(0)
            for ti in range(nt):
                if ti + 2 < nt:
                    tile_loads(ti + 2)
                partB(ti)
                if ti + 1 < nt:
                    partA(ti + 1)
                partC(ti)
            P.barrier()

        tiles0 = [(128 * k, 128 * k) for k in range(T0 // 128)]
        tiles1 = [(b * BH + HALO + 128 * s2, b * BLK + 128 * s2) for b in range(NBLK) for s2 in range(2)]
        tiles2 = [(128 * k, 128 * k) for k in range(T1 // 128)]

        def run():
            if "KONLY" in os.environ:
                stage1(2)
                return
            stage1(0)
            if stop_after == "s1_0":
                return
            stage3(0, xin, h_a, tiles0, False)
            if stop_after == "L0":
                return
            stage1(1)
            stage3(1, h_a, h_b, tiles1, False)
            if stop_after == "L1":
                return
            stage1(2)
            if stop_after == "s1_2":
                dbg_k = nc.dram_tensor("dbg_k", [128, T1], BF16, kind="ExternalOutput").ap()
                dbg_v = nc.dram_tensor("dbg_v", [128, 260], BF16, kind="ExternalOutput").ap()
                dbg_km = nc.dram_tensor("dbg_km", [128, NCORES * NH * NBLK], BF16, kind="ExternalOutput").ap()
                dbg_q = nc.dram_tensor("dbg_q", [128, T1], BF16, kind="ExternalOutput").ap()
                dbg_sg = nc.dram_tensor("dbg_sg", [128, T1], BF16, kind="ExternalOutput").ap()
                AR.off = PERSIST_END
                tk = AR.alloc([T1], BF16)
                tv = AR.alloc([260], BF16)
                dma("sp", tk, kT_all[(3 * NH + 5) * 128:(3 * NH + 5) * 128 + 128, :], (), ["tk"])
                dma("sp", dbg_k, tk, ["tk"], ())
                r0 = ((3 * NH + 5) * NBLK + 2) * 128
                dma("sp", tv, V_all[r0:r0 + 128, :], (), ["tv"])
                dma("sp", dbg_v, tv, ["tv"], ())
                dma("sp", dbg_km, kmT.rearrange("d r x -> d (r x)"), (), ())
                tq = AR.alloc([T1], BF16)
                dma("sp", tq, QT_scr[:, 5, :], (), ["tq"])
                dma("sp", dbg_q, tq, ["tq"], ())
                tsg = AR.alloc([T1], BF16)
                dma("sp", tsg, SG_scr[:, 5, :], (), ["tsg"])
                dma("sp", dbg_sg, tsg, ["tsg"], ())
                return
            stage2(2)
            stage3(2, h_b, h_a, tiles2, False)
            if stop_after == "L2":
                return
            stage1(3)
            stage2(3)
            stage3(3, h_a, out, tiles2, True)

        run()
        nsem = P.emit(st)
        if debug:
            print("ops", len(P.ops), "sems", nsem, flush=True)
    return nc


def make_in_maps(inputs):
    x = np.ascontiguousarray(np.asarray(inputs["x"], dtype=np.float32)[0])
    mem = np.ascontiguousarray(np.asarray(inputs["mem"], dtype=np.float32)[0])
    shared = {
        "mem": mem,
        "w_in": np.ascontiguousarray(inputs["w_in"], dtype=np.float32),
        "w_out": np.ascontiguousarray(inputs["w_out"], dtype=np.float32),
        "w_mem_kv": np.ascontiguousarray(inputs["w_mem_kv"], dtype=np.float32),
        "ln_g": np.ascontiguousarray(inputs["ln_g"], dtype=np.float32),
        "ln_b": np.ascontiguousarray(inputs["ln_b"], dtype=np.float32),
        "pool_w": np.ascontiguousarray(inputs["pool_w"], dtype=np.float32),
        "w_kv_shared": np.ascontiguousarray(inputs["w_kv_shared"], dtype=np.float32),
    }
    ps = np.asarray(inputs["pool_scale"], dtype=np.float32)
    shared["t_pscale"] = np.ascontiguousarray(ps.reshape(2, 12, 128).transpose(2, 0, 1).reshape(128, 24))
    p_ = np.arange(128)[:, None, None]
    c_ = np.arange(2)[None, :, None]
    q_ = np.arange(256)[None, None, :]
    shared["t_tri"] = np.ascontiguousarray(((2 * p_ + c_) <= q_).astype(np.float32).reshape(128, 512))
    maps = []
    for c in range(NCORES):
        xin = np.zeros((T0, D), np.float32)
        for b in range(NBLK):
            g = gblock(c, b)
            lo = g * BLK - HALO
            if lo < 0:
                xin[b * BH + HALO:(b + 1) * BH] = x[0:BLK]
            else:
                xin[b * BH:(b + 1) * BH] = x[lo:lo + BH]
        hmask = np.ones((128, BH), np.float32)
        invcnt = np.zeros((4, BH), np.float32)
        for gi, w in enumerate(POOL_W):
            invcnt[gi, :] = 1.0 / w
        if gblock(c, 0) == 0:
            hmask[:, :HALO] = 0.0
            for gi, w in enumerate(POOL_W):
                t = np.arange(BLK)
                invcnt[gi, HALO:] = 1.0 / np.minimum(t + 1, w)
        invcnt = np.broadcast_to(invcnt.reshape(1, 4 * BH), (128, 4 * BH))
        pm = np.zeros((16, NSLOT), np.float32)
        p01 = np.zeros((16, NSLOT), np.float32)
        for t in range(16):
            gq = gblock(c, t // 2)
            for r in range(NCORES):
                for b in range(NBLK):
                    prior = gblock(r, b) < gq
                    pm[t, r * NBLK + b] = 0.0 if prior else NEG
                    p01[t, r * NBLK + b] = 1.0 if prior else 0.0
        m = dict(shared)
        m["xin"] = xin
        m["t_hmask"] = hmask
        m["t_invcnt"] = np.ascontiguousarray(invcnt)
        m["t_pm"] = np.ascontiguousarray(np.broadcast_to(pm.reshape(1, -1), (128, 16 * NSLOT)))
        m["t_p01"] = np.ascontiguousarray(np.broadcast_to(p01.reshape(1, -1), (128, 16 * NSLOT)))
        maps.append(m)
    return maps


_NC_CACHE = {}


def kernel(**inputs):
    maps = make_in_maps(inputs)
    if "nc" not in _NC_CACHE:
        _NC_CACHE["nc"] = build()
    res = run_bass_kernel_spmd(_NC_CACHE["nc"], maps, core_ids=list(range(NCORES)))
    outp = np.empty((1, SEQ, D), np.float32)
    for c in range(NCORES):
        o = res.results[c]["out"]
        for b in range(NBLK):
            g = gblock(c, b)
            outp[0, g * BLK:(g + 1) * BLK] = o[b * BLK:(b + 1) * BLK]
    return outp
```
